# Optimizing a Trainium2 kernel written in Bass

```python
import math
import jax, jax.numpy as jnp
from jax import lax
import numpy as np

D_MODEL = 1024
BATCH = 4
SEQ = 4096
DEPTH = 1

HEAD_DIM = 64
FOX_HEADS = 8
NSA_HEADS = 8
NSA_KV_GROUPS = 2
NSA_REP = NSA_HEADS // NSA_KV_GROUPS
CMP_BLOCK = 32
CMP_STRIDE = 16
CMP_HIDDEN = 256
SLC_BLOCK = 64
SLC_TOPK = 16
WINDOW = 512
Q_BLOCK = 128
REL_BUCKETS = 32
REL_MAX_DIST = 128
D_FF = 4 * D_MODEL
RMS_EPS = 1e-6
NEG_INF = -1e30
FORCED_SCORE = 1e4
SCALE = HEAD_DIM ** -0.5

FOX_W = FOX_HEADS * HEAD_DIM
NSA_W = NSA_HEADS * HEAD_DIM
KV_W = NSA_KV_GROUPS * HEAD_DIM
IN_SPLITS = (FOX_W, FOX_W, FOX_W, FOX_HEADS,
             NSA_W, KV_W, KV_W, KV_W, KV_W, KV_W, KV_W, 3 * NSA_HEADS,
             D_MODEL, D_MODEL)
IN_WIDTH = sum(IN_SPLITS)

kernel_name = "hybrid_fox_nsa_gated_block"


def rms_norm(x, g):
    xf = x.astype(jnp.float32)
    y = xf * lax.rsqrt(jnp.mean(xf * xf, axis=-1, keepdims=True) + RMS_EPS)
    return (y * g.astype(jnp.float32)).astype(x.dtype)


def masked_softmax(s, mask):
    p = jax.nn.softmax(jnp.where(mask, s, NEG_INF), axis=-1)
    return jnp.where(mask, p, 0.0)


def rel_bucket(dist):
    n = jnp.maximum(dist, 0)
    exact = REL_BUCKETS // 2
    nf = jnp.maximum(n, exact).astype(jnp.float32)
    log_b = exact + (jnp.log(nf / exact) / math.log(REL_MAX_DIST / exact)
                     * (REL_BUCKETS - exact)).astype(jnp.int32)
    return jnp.where(n < exact, n, jnp.minimum(log_b, REL_BUCKETS - 1))


def fox_attention(q, k, v, f_logit, b_f):
    B, T, H, DH = q.shape
    nb = T // Q_BLOCK
    log_f = jax.nn.log_sigmoid(f_logit.astype(jnp.float32) + b_f.astype(jnp.float32))
    c = jnp.cumsum(log_f, axis=1).transpose(0, 2, 1)
    qb = q.reshape(B, nb, Q_BLOCK, H, DH).transpose(1, 0, 3, 2, 4)
    cqb = c.reshape(B, H, nb, Q_BLOCK).transpose(2, 0, 1, 3)
    kpos = jnp.arange(T)

    def block(args):
        i, qi, ci = args
        qpos = i * Q_BLOCK + jnp.arange(Q_BLOCK)
        s = jnp.einsum('bhqd,bkhd->bhqk', qi, k).astype(jnp.float32) * SCALE
        s = s + ci[..., None] - c[:, :, None, :]
        p = masked_softmax(s, kpos[None, :] <= qpos[:, None])
        return jnp.einsum('bhqk,bkhd->bqhd', p.astype(v.dtype), v)

    o = lax.map(block, (jnp.arange(nb), qb, cqb))
    return o.transpose(1, 0, 2, 3, 4).reshape(B, T, H * DH)


def compress_blocks(kv, pe, w1, w2):
    B, T, G, DH = kv.shape
    nc = (T - CMP_BLOCK) // CMP_STRIDE + 1
    idx = np.arange(nc)[:, None] * CMP_STRIDE + np.arange(CMP_BLOCK)[None, :]
    blk = kv[:, idx] + pe[:, None, :]
    blk = blk.transpose(0, 3, 1, 2, 4).reshape(B, G, nc, CMP_BLOCK * DH)
    return jax.nn.silu(blk @ w1) @ w2


def nsa_attention(q, k_cmp, v_cmp, k_slc, v_slc, k_win, v_win, gate_logit, rel_table,
                  pe_k, w1_k, w2_k, pe_v, w1_v, w2_v):
    B, T, H, DH = q.shape
    G, R = NSA_KV_GROUPS, NSA_REP
    nb = T // Q_BLOCK
    nc = (T - CMP_BLOCK) // CMP_STRIDE + 1
    ns = T // SLC_BLOCK
    n_sel = min(SLC_TOPK, ns)

    kc = compress_blocks(k_cmp, pe_k, w1_k, w2_k)
    vc = compress_blocks(v_cmp, pe_v, w1_v, w2_v)
    cmp_start = np.arange(nc) * CMP_STRIDE
    cmp_end = jnp.asarray(cmp_start + CMP_BLOCK - 1, dtype=jnp.int32)
    slc_start = np.arange(ns) * SLC_BLOCK
    overlap = jnp.asarray(((cmp_start[:, None] < slc_start[None, :] + SLC_BLOCK)
                           & (cmp_start[:, None] + CMP_BLOCK > slc_start[None, :])).astype(np.float32))

    tbl_heads = rel_table.T
    tbl_grp = tbl_heads.reshape(G, R, REL_BUCKETS)
    g_idx = jnp.arange(G)[None, :, None, None, None]
    r_idx = jnp.arange(R)[None, None, :, None, None]

    ks = k_slc.transpose(0, 2, 1, 3).reshape(B, G, ns, SLC_BLOCK, DH)
    vs = v_slc.transpose(0, 2, 1, 3).reshape(B, G, ns, SLC_BLOCK, DH)
    pad = ((0, 0), (0, 0), (WINDOW, 0), (0, 0))
    kwp = jnp.pad(k_win.transpose(0, 2, 1, 3), pad)
    vwp = jnp.pad(v_win.transpose(0, 2, 1, 3), pad)
    win_dist = jnp.asarray(np.arange(Q_BLOCK)[:, None] + WINDOW - np.arange(Q_BLOCK + WINDOW)[None, :],
                           dtype=jnp.int32)
    bias_win = tbl_heads[:, rel_bucket(win_dist)].reshape(G, R, Q_BLOCK, Q_BLOCK + WINDOW)

    qb = q.reshape(B, nb, Q_BLOCK, G, R, DH).transpose(1, 0, 3, 4, 2, 5)
    b_idx = jnp.arange(B)[:, None, None, None]
    gk_idx = jnp.arange(G)[None, :, None, None]
    j = jnp.arange(ns)

    def block(args):
        i, qi = args
        qpos = i * Q_BLOCK + jnp.arange(Q_BLOCK)
        s = jnp.einsum('bgrqd,bgcd->bgrqc', qi, kc).astype(jnp.float32) * SCALE
        bias = tbl_heads[:, rel_bucket(qpos[:, None] - cmp_end[None, :])].reshape(G, R, Q_BLOCK, nc)
        p_cmp = masked_softmax(s + bias, cmp_end[None, :] <= qpos[:, None])
        o_cmp = jnp.einsum('bgrqc,bgcd->bgrqd', p_cmp.astype(vc.dtype), vc)
        imp = jnp.einsum('bgrqc,cs->bgqs', p_cmp, overlap)
        cur = qpos // SLC_BLOCK
        forced = (j[None, :] == 0) | (j[None, :] == cur[:, None]) | (j[None, :] == cur[:, None] - 1)
        valid = j[None, :] * SLC_BLOCK <= qpos[:, None]
        imp = jnp.where(forced, FORCED_SCORE, jnp.where(valid, imp, -1.0))
        _, sel = lax.top_k(imp, n_sel)
        k_sel = ks[b_idx, gk_idx, sel].reshape(B, G, Q_BLOCK, n_sel * SLC_BLOCK, DH)
        v_sel = vs[b_idx, gk_idx, sel].reshape(B, G, Q_BLOCK, n_sel * SLC_BLOCK, DH)
        pos = (sel[..., None] * SLC_BLOCK + jnp.arange(SLC_BLOCK)).reshape(B, G, Q_BLOCK, n_sel * SLC_BLOCK)
        s = jnp.einsum('bgrqd,bgqkd->bgrqk', qi, k_sel).astype(jnp.float32) * SCALE
        bias = tbl_grp[g_idx, r_idx, rel_bucket(qpos[:, None] - pos)[:, :, None]]
        p = masked_softmax(s + bias, (pos <= qpos[:, None])[:, :, None])
        o_slc = jnp.einsum('bgrqk,bgqkd->bgrqd', p.astype(v_sel.dtype), v_sel)
        kw = lax.dynamic_slice_in_dim(kwp, i * Q_BLOCK, Q_BLOCK + WINDOW, axis=2)
        vw = lax.dynamic_slice_in_dim(vwp, i * Q_BLOCK, Q_BLOCK + WINDOW, axis=2)
        kpos = i * Q_BLOCK - WINDOW + jnp.arange(Q_BLOCK + WINDOW)
        wmask = ((kpos[None, :] >= 0) & (kpos[None, :] <= qpos[:, None])
                 & (qpos[:, None] - kpos[None, :] < WINDOW))
        s = jnp.einsum('bgrqd,bgkd->bgrqk', qi, kw).astype(jnp.float32) * SCALE + bias_win
        p = masked_softmax(s, wmask)
        o_win = jnp.einsum('bgrqk,bgkd->bgrqd', p.astype(vw.dtype), vw)
        return jnp.stack([o_cmp, o_slc, o_win], axis=-2)

    o = lax.map(block, (jnp.arange(nb), qb))
    o = o.transpose(1, 0, 4, 2, 3, 5, 6).reshape(B, T, H, 3, DH)
    gates = jax.nn.sigmoid(gate_logit)[..., None]
    return jnp.sum(gates * o, axis=3).reshape(B, T, H * DH)


def setup_inputs(seed: int = 0) -> dict:
    key = jax.random.key(seed)
    ks = jax.random.split(key, 20)

    def w(k, shape, fan_in):
        return jax.random.normal(k, shape, jnp.float32) * fan_in ** -0.5

    def gain(k, shape):
        return 1.0 + 0.05 * jax.random.normal(k, shape, jnp.float32)

    nrm = lambda k, shape: jax.random.normal(k, shape, jnp.float32)
    L = DEPTH
    return {
        "x": nrm(ks[0], (BATCH, SEQ, D_MODEL)),
        "rel_bias_table": 0.5 * nrm(ks[1], (REL_BUCKETS, NSA_HEADS)),
        "g_attn": gain(ks[2], (L, D_MODEL)),
        "w_in": w(ks[3], (L, D_MODEL, IN_WIDTH), D_MODEL),
        "b_forget": 3.0 + 0.5 * nrm(ks[4], (L, FOX_HEADS)),
        "cmp_pe_k": 0.5 * nrm(ks[5], (L, CMP_BLOCK, HEAD_DIM)),
        "cmp_w1_k": w(ks[6], (L, CMP_BLOCK * HEAD_DIM, CMP_HIDDEN), CMP_BLOCK * HEAD_DIM),
        "cmp_w2_k": w(ks[7], (L, CMP_HIDDEN, HEAD_DIM), CMP_HIDDEN),
        "cmp_pe_v": 0.5 * nrm(ks[8], (L, CMP_BLOCK, HEAD_DIM)),
        "cmp_w1_v": w(ks[9], (L, CMP_BLOCK * HEAD_DIM, CMP_HIDDEN), CMP_BLOCK * HEAD_DIM),
        "cmp_w2_v": w(ks[10], (L, CMP_HIDDEN, HEAD_DIM), CMP_HIDDEN),
        "w_br_fox": w(ks[11], (L, FOX_W, D_MODEL), FOX_W),
        "w_br_nsa": w(ks[12], (L, NSA_W, D_MODEL), NSA_W),
        "w_out": w(ks[13], (L, D_MODEL, D_MODEL), D_MODEL),
        "g_mlp": gain(ks[14], (L, D_MODEL)),
        "w_ff1": w(ks[15], (L, D_MODEL, D_FF), D_MODEL),
        "w_ff2": w(ks[16], (L, D_FF, D_MODEL), D_FF),
        "g_final": gain(ks[17], (D_MODEL,)),
    }


def reference(x, rel_bias_table, g_attn, w_in, b_forget, cmp_pe_k, cmp_w1_k, cmp_w2_k,
              cmp_pe_v, cmp_w1_v, cmp_w2_v, w_br_fox, w_br_nsa, w_out, g_mlp, w_ff1, w_ff2,
              g_final):
    B, T, _ = x.shape
    offsets = np.cumsum(IN_SPLITS)[:-1].tolist()
    for l in range(DEPTH):
        h = rms_norm(x, g_attn[l])
        z = h @ w_in[l]
        (fq, fk, fv, f_logit, nq, kc, vc, ksl, vsl, kwn, vwn, n_gate, g_a, g_b) = jnp.split(z, offsets, axis=-1)
        fox_h = lambda t: t.reshape(B, T, FOX_HEADS, HEAD_DIM)
        kv_h = lambda t: t.reshape(B, T, NSA_KV_GROUPS, HEAD_DIM)
        o_fox = fox_attention(fox_h(fq), fox_h(fk), fox_h(fv), f_logit, b_forget[l])
        o_nsa = nsa_attention(nq.reshape(B, T, NSA_HEADS, HEAD_DIM), kv_h(kc), kv_h(vc), kv_h(ksl), kv_h(vsl),
                              kv_h(kwn), kv_h(vwn), n_gate.reshape(B, T, NSA_HEADS, 3), rel_bias_table,
                              cmp_pe_k[l], cmp_w1_k[l], cmp_w2_k[l], cmp_pe_v[l], cmp_w1_v[l], cmp_w2_v[l])
        merged = jax.nn.sigmoid(g_a) * (o_fox @ w_br_fox[l]) + jax.nn.sigmoid(g_b) * (o_nsa @ w_br_nsa[l])
        x = x + merged @ w_out[l]
        h = rms_norm(x, g_mlp[l])
        x = x + jnp.square(jax.nn.relu(h @ w_ff1[l])) @ w_ff2[l]
    return rms_norm(x, g_final)
```

```python
import math
import contextlib
import numpy as np
import concourse.bass as bass
import concourse.mybir as mybir
from concourse.bass_utils import run_bass_kernel_spmd

F32 = mybir.dt.float32
BF16 = mybir.dt.bfloat16
ALU = mybir.AluOpType
AF = mybir.ActivationFunctionType
AX = mybir.AxisListType

ENGINES = ["tensor", "vector", "scalar", "gpsimd", "sync"]
NDMA_SEM = 8
SCALE = 0.125
EPS = 1e-6
BIG = 30000.0


class _Rec:
    def __getattr__(self, name):
        def rec(*a, **kw):
            return lambda e: getattr(e, name)(*a, **kw)
        return rec


I = _Rec()


class Prog:
    def __init__(self, nc):
        self.nc = nc
        self.instrs = []

    def op(self, eng, fn, reads=(), writes=(), dma=False):
        self.instrs.append(dict(eng=eng, fn=fn, reads=tuple(reads), writes=tuple(writes), dma=dma, bar=False))
        return len(self.instrs) - 1

    def barrier(self):
        self.instrs.append(dict(eng=None, fn=None, reads=(), writes=(), dma=False, bar=True))

    def mm(self, out, lhsT, rhs, start, stop, reads, writes):
        return self.op("tensor", I.matmul(out, lhsT, rhs, start=start, stop=stop,
                                                     skip_group_check=True), reads, writes)

    def tr(self, out, in_, ident, reads, writes):
        return self.op("tensor", I.transpose(out, in_, ident), reads, writes)

    def dma(self, eng, out, in_, reads, writes, **kw):
        return self.op(eng, I.dma_start(out=out, in_=in_, **kw), reads, writes, dma=True)

    def emit(self):
        nc = self.nc
        instrs = self.instrs
        n = len(instrs)
        last_writer, readers, group_deps = {}, {}, {}
        deps = [set() for _ in range(n)]
        has_dep = [False] * n
        last_on_eng = {}
        last_dmas = {}
        pending_bar = {}
        dma_count = {e: 0 for e in ENGINES}
        dma_key = [None] * n
        prev_same_dma = [None] * n
        last_on_dma_sem = {}
        for i, ins in enumerate(instrs):
            if ins["bar"]:
                allprev = set(last_on_eng.values()) | set(last_on_dma_sem.values())
                for e in ENGINES:
                    pending_bar[e] = set(allprev)
                continue
            e = ins["eng"]
            d = set()
            for r in ins["reads"]:
                d.update(last_writer.get(r, ()))
            joined = set()
            for r in ins["writes"]:
                lw = last_writer.get(r, [])
                if (ins["dma"] and lw and not readers.get(r) and all(instrs[w]["dma"] for w in lw)):
                    d.update(group_deps.get(r, ()))
                    joined.add(r)
                else:
                    gd = set(lw) | set(readers.get(r, {}).values())
                    d.update(gd)
                    group_deps[r] = gd
            for r in ins["reads"]:
                rk = ("dma", i) if ins["dma"] else e
                readers.setdefault(r, {})[rk] = i
            for r in ins["writes"]:
                if r in joined:
                    last_writer[r].append(i)
                else:
                    last_writer[r] = [i]
                    readers[r] = {}
            d.discard(i)
            if e == "tensor" and not ins["dma"]:
                d = {j for j in d if not (instrs[j]["eng"] == "tensor" and not instrs[j]["dma"])}
            if e in pending_bar:
                d.update(pending_bar.pop(e))
            if ins["dma"]:
                k = dma_count[e]
                dma_count[e] += 1
                key = ("dma", e, k % NDMA_SEM)
                dma_key[i] = (key, 16 * (k // NDMA_SEM + 1))
                prev_same_dma[i] = last_on_dma_sem.get(key)
                last_on_dma_sem[key] = i
            else:
                last_on_eng[e] = i
            deps[i] = d
            for j in d:
                has_dep[j] = True
        eng_count = {e: 0 for e in ENGINES}
        ticket = [None] * n
        for i, ins in enumerate(instrs):
            if ins["bar"]:
                continue
            if ins["dma"]:
                ticket[i] = dma_key[i]
            elif has_dep[i]:
                eng_count[ins["eng"]] += 1
                ticket[i] = (("eng", ins["eng"]), eng_count[ins["eng"]])
        self.stats = dict(n=n, eng_count=dict(eng_count), dma_count=dict(dma_count))
        with contextlib.ExitStack() as st:
            sems = {}
            for e in ENGINES:
                sems[("eng", e)] = st.enter_context(nc.semaphore(f"s_{e}"))
                for k in range(NDMA_SEM):
                    sems[("dma", e, k)] = st.enter_context(nc.semaphore(f"d_{e}_{k}"))
            block = st.enter_context(nc.Block())
            per_eng = {e: [i for i in range(n) if instrs[i]["eng"] == e] for e in ENGINES}
            final_vals = {}
            for i in range(n):
                if ticket[i] is not None:
                    key, v = ticket[i]
                    final_vals[key] = max(final_vals.get(key, 0), v)

            def make(e):
                def body(eng):
                    waited = {}
                    for i in per_eng[e]:
                        ins = instrs[i]
                        need = {}
                        for j in deps[i]:
                            key, v = ticket[j]
                            need[key] = max(need.get(key, 0), v)
                        if ins["dma"] and prev_same_dma[i] is not None:
                            key, v = ticket[prev_same_dma[i]]
                            need[key] = max(need.get(key, 0), v)
                        for key, v in need.items():
                            if waited.get(key, 0) < v:
                                eng.wait_ge(sems[key], v)
                                waited[key] = v
                        bi = ins["fn"](eng)
                        if ticket[i] is not None:
                            key, v = ticket[i]
                            bi.then_inc(sems[key], 16 if ins["dma"] else 1)
                    for key, v in final_vals.items():
                        if key[0] == "dma" and key[1] == e and waited.get(key, 0) < v:
                            eng.wait_ge(sems[key], v)
                return body

            for e in ENGINES:
                if per_eng[e]:
                    getattr(block, e)(make(e))
        return self.stats


def _rel_bucket(dist):
    n = np.maximum(dist, 0)
    nf = np.maximum(n, 16).astype(np.float32)
    log_b = 16 + (np.log(nf / np.float32(16)) / np.float32(math.log(128 / 16)) * np.float32(16)).astype(np.int32)
    return np.where(n < 16, n, np.minimum(log_b, 31)).astype(np.int64)


def _onehot_dist(dist, valid):
    oh = np.zeros((32, dist.size), np.float32)
    b = _rel_bucket(dist)
    idx = np.nonzero(valid)[0]
    oh[b[idx], idx] = 1.0
    return oh


def make_consts(par):
    c = {}
    c["ident"] = np.eye(128, dtype=np.float32)
    s = np.arange(128)[:, None]
    t = np.arange(128)[None, :]
    tri = (s <= t).astype(np.float32)
    ones = np.ones((128, 128), np.float32)
    zeros = np.zeros((128, 128), np.float32)
    c["m0"] = tri if par == 0 else ones
    c["m1"] = zeros if par == 0 else tri
    ind = np.zeros((64, 4096), np.float32)
    ind[np.arange(4096) // 64, np.arange(4096)] = BIG
    c["ind"] = ind
    dw = np.arange(1023) - 255 + par * 128
    vw = (dw >= 0) & (dw < 512)
    c["ohw"] = _onehot_dist(dw, vw)
    c["wmask"] = np.broadcast_to(vw.astype(np.float32)[None, :], (128, 1023)).copy()
    ds = np.arange(511) - 255 + par * 128
    vs = ds >= 0
    c["ohs"] = _onehot_dist(ds, vs)
    c["smask"] = np.broadcast_to(vs.astype(np.float32)[None, :], (128, 511)).copy()
    q = np.arange(128)
    ohc = np.zeros((32, 25 * 128), np.float32)
    cm = np.zeros((128, 25), np.float32)
    for k in range(25):
        d = par * 128 + q - 16 * (k - 10) - 31
        v = d >= 0
        ohc[:, k * 128:(k + 1) * 128] = _onehot_dist(d, v)
        cm[:, k] = np.where(v, 0.0, -BIG)
    c["ohc"] = ohc
    c["cm"] = cm
    vmadd = np.zeros((16, 128, 128), np.float32)
    j = np.arange(64)[None, :]
    for g in range(16):
        qpos = (2 * g + par) * 128 + np.arange(128)[:, None]
        cur = qpos // 64
        forced = (j == 0) | (j == cur) | (j == cur - 1)
        valid = j * 64 <= qpos
        vm = (valid & ~forced).astype(np.float32)
        add = np.where(forced, 1e4, np.where(valid, 0.0, -1.0)).astype(np.float32)
        vmadd[g, :, 0:64] = vm
        vmadd[g, :, 64:128] = add
    c["vmadd"] = vmadd
    k_ = np.arange(128)[:, None]
    p_ = np.arange(128)[None, :]
    c["tri"] = (k_ <= p_).astype(np.float32)
    c["ones"] = ones.copy()
    return c


IN_SHAPES = dict(
    xf=[4096, 1024], xo=[2048, 1024], wkv=[1024, 1800], wq=[1024, 1048], wg=[1024, 2048],
    gat=[128, 8], gml=[128, 8], gfin=[1, 1024], bfg=[1, 8], tbl=[32, 8],
    pek=[32, 64], pev=[32, 64], w1k=[2048, 256], w1v=[2048, 256], w2k=[256, 64], w2v=[256, 64],
    wbf=[512, 1024], wbn=[512, 1024], wo=[1024, 1024], wf1=[1024, 4096], wf2=[4096, 1024],
    ident=[128, 128], m0=[128, 128], m1=[128, 128], ind=[64, 4096], ohw=[32, 1023], wmask=[128, 1023],
    ohs=[32, 511], smask=[128, 511], ohc=[32, 3200], cm=[128, 25], vmadd=[16, 128, 128],
    tri=[128, 128], ones=[128, 128],
)


class _Stop(Exception):
    pass


def build(nc, upto=None, nslots=16):
    P = Prog(nc)
    dumps = {}

    def dump(name, ap, shape, keys, dt=F32):
        if upto is None:
            return
        t = nc.dram_tensor("dbg_" + name, list(shape), dt, kind="ExternalOutput").ap()
        P.dma("sync", t, ap, keys, [])
        dumps[name] = shape

    def phase_end(name):
        if upto == name:
            raise _Stop()

    st = contextlib.ExitStack()
    try:
        _build_body(nc, P, dump, phase_end, nslots, st)
    except _Stop:
        pass
    stats = P.emit()
    st.close()
    stats["dumps"] = dumps
    return stats


def _build_body(nc, P, dump, phase_end, nslots, st):
    D = {k: nc.dram_tensor(k, list(v), F32, kind="ExternalInput").ap() for k, v in IN_SHAPES.items()}
    out = nc.dram_tensor("out", [2048, 1024], F32, kind="ExternalOutput").ap()
    ebs_w = nc.dram_tensor("ebs_w", [8 * 128, 1023], BF16, kind="Internal")
    ebs_s = nc.dram_tensor("ebs_s", [8 * 128, 511], BF16, kind="Internal")
    dbg_out = {}

    ARENA_BYTES = 212800
    arena = st.enter_context(nc.sbuf_tensor("arena", [128, ARENA_BYTES // 2], BF16))
    banks = [st.enter_context(nc.psum_tensor(f"bank{i}", [128, 512], F32)) for i in range(8)]

    class Alloc:
        def __init__(self, lo, hi):
            self.lo, self.hi, self.cur = lo, hi, lo

        def get(self, free_shape, dt):
            nel = int(np.prod(free_shape))
            nb = nel * (4 if dt == F32 else 2)
            nb = (nb + 63) // 64 * 64
            off = self.cur
            self.cur += nb
            assert self.cur <= self.hi, ("arena overflow", self.cur, self.hi)
            v = arena[:, off // 2:(off + nel * (4 if dt == F32 else 2)) // 2]
            if dt == F32:
                v = v.bitcast(F32)
            if len(free_shape) == 2:
                v = v.rearrange("p (a b) -> p a b", a=free_shape[0])
            elif len(free_shape) == 3:
                v = v.rearrange("p (a b c) -> p a b c", a=free_shape[0], b=free_shape[1])
            return v

    def bankv(i, dt=F32):
        b = banks[i][:, :]
        return b if dt == F32 else b.bitcast(BF16)

    A0 = Alloc(0, ARENA_BYTES)
    identb = A0.get([128], BF16)
    m0b = A0.get([128], BF16)
    m1b = A0.get([128], BF16)
    gat = A0.get([8], F32)
    gml = A0.get([8], F32)
    EPS_AP = A0.get([1], F32)
    ONE_AP = A0.get([1], F32)
    misc_end = A0.cur
    KF = A0.get([4, 4096], BF16)
    VF = A0.get([32, 8, 65], BF16)
    KS0 = A0.get([4096], BF16)
    KS1 = A0.get([4096], BF16)
    VS = A0.get([32, 2, 65], BF16)
    KW = A0.get([4096], BF16)
    VW = A0.get([32, 2, 65], BF16)
    NCt = A0.get([32, 8], F32)
    NCoff = A0.get([33, 8], F32)
    KCc = A0.get([256], BF16)
    VCA = A0.get([2, 2, 65], BF16)
    T31 = A0.get([8], F32)
    BA = A0.get([8, 25], F32)
    kv_end = A0.cur
    R0 = kv_end

    def load_const_bf(dst, src, key):
        P.dma("gpsimd", dst, src, [], [key])

    load_const_bf(identb, D["ident"][:, :], "identb")
    load_const_bf(m0b, D["m0"][:, :], "m0b")
    load_const_bf(m1b, D["m1"][:, :], "m1b")
    P.dma("sync", gat, D["gat"][:, :], [], ["gat"])
    P.dma("sync", gml, D["gml"][:, :], [], ["gml"])
    P.op("vector", I.memset(VF[:, :, :, 64:65], 1.0), [], ["VFone"])
    P.op("vector", I.memset(VS[:, :, :, 64:65], 1.0), [], ["VSone"])
    P.op("vector", I.memset(VW[:, :, :, 64:65], 1.0), [], ["VWone"])
    P.op("vector", I.memset(VCA[:, :, :, :], 0.0), [], ["VCA"])
    P.op("vector", I.memset(VCA[:, :, :, 64:65], 1.0), ["VCA"], ["VCA"])
    P.op("vector", I.memset(KCc[:, :], 0.0), [], ["KCc"])

    def norm_transpose(xsrc_ap, xt, sq, ss, rstd, xn, hT_dst, gcol, tag, psb, rkeys, wkey, dma=True, on_dve=False, on_pool=False):
        if dma:
            P.dma("sync", xt, xsrc_ap, [], [tag + "xt"])
        if on_dve:
            P.op("vector", I.scalar_tensor_tensor(xn, xt, 1.0, xt, ALU.mult, ALU.mult, accum_out=ss), [tag + "xt"], [tag + "xn", tag + "ss"])
        else:
            P.op("scalar", I.activation(xn, xt, AF.Square, accum_out=ss), [tag + "xt"], [tag + "xn", tag + "ss"])
        P.op("scalar", I.activation(rstd, ss, AF.Ln, bias=EPS_AP, scale=1.0 / 1024), [tag + "ss", "epsap"], [tag + "rms"])
        P.op("scalar", I.activation(rstd, rstd, AF.Exp, scale=-0.5), [tag + "rms"], [tag + "rstd"])
        if on_dve or on_pool:
            P.op("vector" if on_dve else "gpsimd", I.tensor_scalar(xn, xt, rstd, None, ALU.mult), [tag + "xt", tag + "rstd"], [tag + "xn"])
        else:
            P.op("scalar", I.activation(xn, xt, AF.Copy, scale=rstd), [tag + "xt", tag + "rstd"], [tag + "xn"])
        pb = bankv(psb, BF16)
        for d in range(8):
            P.tr(pb[:, d * 128:(d + 1) * 128], xn[:, d * 128:(d + 1) * 128], identb, [tag + "xn", "identb"], [f"B{psb}"])
        pv = pb.rearrange("p (d t) -> p d t", d=8)
        gb = gcol.rearrange("p (d o) -> p d o", o=1).broadcast_to([128, 8, 128])
        P.op("vector", I.tensor_tensor(hT_dst, pv, gb, ALU.mult), [f"B{psb}", "gat", "gml"] + list(rkeys), [wkey])

    P.op("vector", I.memset(EPS_AP, EPS), [], ["epsap"])
    P.op("vector", I.memset(ONE_AP, 1.0), [], ["oneap"])
    R0 = A0.cur

    AA = Alloc(R0, ARENA_BYTES)
    KCT = AA.get([4096], BF16)
    VCT = AA.get([4096], BF16)
    FLOG = AA.get([32, 8], F32)
    a_scratch = AA.cur
    WKV = AA.get([8, 1800], BF16)
    xts = [AA.get([1024], F32) for _ in range(4)]
    sqs = None
    xns = [AA.get([1024], BF16) for _ in range(4)]
    sss = [AA.get([1], F32) for _ in range(4)]
    rstds = [AA.get([1], F32) for _ in range(4)]
    hTs = [AA.get([8, 512], BF16) for _ in range(2)]

    wkv_v = D["wkv"].rearrange("(c p) n -> p c n", p=128)
    for d in range(8):
        P.dma("gpsimd", WKV[:, d, 1024:1800], wkv_v[:, d, 1024:1800], [], [f"WKV{d}b"])
    for d in range(8):
        P.dma("gpsimd", WKV[:, d, 0:1024], wkv_v[:, d, 0:1024], [], [f"WKV{d}a"])
    WKVK = [f"WKV{d}a" for d in range(8)]
    load_const_bf(KS0[64:128, :], D["ind"][:, :], "KS0i")
    load_const_bf(KS1[0:64, :], D["ind"][:, :], "KS1i")
    WKVKB = [f"WKV{d}b" for d in range(8)]
    CONV = {}
    conv_jobs = []
    for nm, (r_, c_) in (("wg", (1024, 2048)), ("wbf", (512, 1024)), ("wbn", (512, 1024)), ("wo", (1024, 1024)),
                         ("wf1", (1024, 4096)), ("wf2", (4096, 1024))):
        dst = nc.dram_tensor(nm + "_bf", [r_, c_], BF16, kind="Internal").ap()
        CONV[nm] = dst
        if c_ >= 2048:
            sv = D[nm].rearrange("r (a c) -> (r a) c", c=2048)
            dv = dst.rearrange("r (a c) -> (r a) c", c=2048)
        else:
            sv = D[nm].rearrange("(r a) c -> r (a c)", a=2048 // c_)
            dv = dst.rearrange("(r a) c -> r (a c)", a=2048 // c_)
        for i in range(sv.shape[0] // 128):
            conv_jobs.append((dv[i * 128:(i + 1) * 128, :], sv[i * 128:(i + 1) * 128, :], nm + "_bf"))

    xf_v = D["xf"].rearrange("(j p) n -> j p n", p=128)
    def nt_a(j):
        stl_, tt_ = j // 4, j % 4
        b_ = j % 4
        norm_transpose(xf_v[j], xts[b_], sqs, sss[b_], rstds[b_], xns[b_],
                       hTs[stl_ % 2][:, :, tt_ * 128:(tt_ + 1) * 128], gat, f"A{b_}", 7, [], f"hT{stl_ % 2}")

    nt_a(0)
    nt_a(1)
    for stl in range(8):
        hT = hTs[stl % 2]
        hkey = f"hT{stl % 2}"
        for tt in range(4):
            j = stl * 4 + tt
            b = j % 2
            for d in range(8):
                P.mm(bankv(0)[:, 0:512], hT[:, d, tt * 128:(tt + 1) * 128], WKV[:, d, 1024:1536], d == 0, d == 7,
                     [hkey] + WKVKB, ["B0"])
            for d in range(8):
                P.mm(bankv(1)[:, 0:264], hT[:, d, tt * 128:(tt + 1) * 128], WKV[:, d, 1536:1800], d == 0, d == 7,
                     [hkey] + WKVKB, ["B1"])
            if j + 2 < 32:
                nt_a(j + 2)
            P.op("scalar", I.activation(VF[:, j, :, 0:64], bankv(0)[:, 0:512].rearrange("p (h d) -> p h d", h=8), AF.Copy),
                 ["B0"], [("VF", j)])
            P.op("vector", I.tensor_copy(VS[:, j, :, 0:64], bankv(1)[:, 0:128].rearrange("p (h d) -> p h d", h=2)),
                 ["B1"], [("VS", j)])
            P.op("vector", I.tensor_copy(VW[:, j, :, 0:64], bankv(1)[:, 128:256].rearrange("p (h d) -> p h d", h=2)),
                 ["B1"], [("VW", j)])
            P.op("vector", I.tensor_copy(FLOG[:, j, :], bankv(1)[:, 256:264]), ["B1"], ["FLOG"])
        tsl = slice(stl * 512, (stl + 1) * 512)
        fm = [("fk", 0), ("fk", 1), ("fk", 2), ("fk", 3), ("kc", 4), ("vc", 5), ("ksl", 6), ("kwn", 7)]
        for n_, (kind, cc) in enumerate(fm):
            bk = 2 + (n_ % 2)
            for d in range(8):
                P.mm(bankv(bk)[:, 0:512], WKV[:, d, cc * 128:(cc + 1) * 128], hT[:, d, :], d == 0, d == 7,
                     [hkey] + WKVK, [f"B{bk}"])
            src = bankv(bk)[:, 0:512]
            if kind == "fk":
                P.op("scalar" if n_ % 2 else "vector",
                     (I.activation(KF[:, cc, tsl], src, AF.Copy)) if n_ % 2 else
                     (I.tensor_copy(KF[:, cc, tsl], src)),
                     [f"B{bk}"], [("KF", stl)])
            elif kind == "kc":
                P.op("vector", I.tensor_copy(KCT[:, tsl], src), [f"B{bk}"], ["KCT"])
            elif kind == "vc":
                P.op("scalar", I.activation(VCT[:, tsl], src, AF.Copy), [f"B{bk}"], ["VCT"])
            elif kind == "ksl":
                P.op("vector", I.tensor_copy(KS0[0:64, tsl], src[0:64, :]), [f"B{bk}"], [("KS0", stl)])
                P.op("vector", I.tensor_copy(KS1[64:128, tsl], src[64:128, :]), [f"B{bk}"], [("KS1", stl)])
            else:
                P.op("scalar", I.activation(KW[:, tsl], src, AF.Copy), [f"B{bk}"], [("KW", stl)])

    dump("KF", KF, [128, 4, 4096], [("KF", i) for i in range(8)], BF16)
    dump("VF", VF, [128, 32, 8, 65], [("VF", i) for i in range(32)] + ["VFone"], BF16)
    dump("KS0", KS0, [128, 4096], [("KS0", i) for i in range(8)] + ["KS0i"], BF16)
    dump("KW", KW, [128, 4096], [("KW", i) for i in range(8)], BF16)
    dump("FLOG", FLOG, [128, 32, 8], ["FLOG"])
    dump("KCT", KCT, [128, 4096], ["KCT"], BF16)
    phase_end("A")
    P.barrier()
    AD = Alloc(a_scratch, ARENA_BYTES)
    trif = AD.get([128], F32)
    onesf = AD.get([128], F32)
    bfrep = AD.get([8], F32)
    nlf = AD.get([32, 8], F32)
    P.dma("sync", trif, D["tri"][:, :], [], ["trif"])
    P.dma("sync", onesf, D["ones"][:, :], [], ["onesf"])
    P.dma("sync", bfrep, D["bfg"].partition_broadcast(128)[:, 0, :], [], ["bfrep"])
    P.op("vector", I.tensor_tensor(nlf, FLOG, bfrep.rearrange("p (o h) -> p o h", o=1).broadcast_to([128, 32, 8]), ALU.add),
         ["FLOG", "bfrep"], ["nlf"])
    P.op("scalar", I.activation(nlf, nlf, AF.Exp, scale=-1.0), ["nlf"], ["nlf"])
    P.op("scalar", I.activation(nlf, nlf, AF.Ln, bias=ONE_AP, scale=1.0), ["nlf", "oneap"], ["nlf"])
    nlf2 = nlf.rearrange("p j h -> p (j h)")
    P.mm(bankv(0)[:, 0:256], trif, nlf2, True, True, ["trif", "nlf"], ["B0"])
    P.mm(bankv(1)[:, 0:256], onesf, nlf2, True, True, ["onesf", "nlf"], ["B1"])
    tot = AD.get([32, 8], F32)
    P.op("vector", I.tensor_copy(tot, bankv(1)[:, 0:256].rearrange("p (j h) -> p j h", j=32)), ["B1"], ["tot"])
    P.op("vector", I.memset(NCoff[:, 0, :], 0.0), [], ["NCoff"])
    for j in range(32):
        P.op("vector", I.tensor_tensor(NCoff[:, j + 1, :], NCoff[:, j, :], tot[:, j, :], ALU.add),
             ["NCoff", "tot"], ["NCoff"])
    P.op("vector", I.tensor_tensor(NCt, bankv(0)[:, 0:256].rearrange("p (j h) -> p j h", j=32), NCoff[:, 0:32, :], ALU.add),
         ["B0", "NCoff"], ["NCt"])

    dump("NCt", NCt, [128, 32, 8], ["NCt"])
    dump("NCoff", NCoff, [128, 33, 8], ["NCoff"])
    phase_end("D")
    AC = Alloc(AD.cur, ARENA_BYTES)
    W1K = AC.get([32, 256], BF16)
    W1V = AC.get([32, 256], BF16)
    W2K2 = AC.get([2, 128], BF16)
    W2V = AC.get([2, 64], BF16)
    pef = AC.get([2, 32], F32)
    peb = AC.get([2, 32], BF16)
    peW = AC.get([2, 2], F32)
    a1T = AC.get([4, 2, 256], BF16)
    KCTb = AC.get([16, 256], BF16)
    VCTb = AC.get([16, 256], BF16)
    P.op("vector", I.tensor_copy(KCTb, KCT.rearrange("p (c b) -> p b c", b=16)), ["KCT"], ["KCTb"])
    P.op("gpsimd", I.tensor_copy(VCTb, VCT.rearrange("p (c b) -> p b c", b=16)), ["VCT"], ["VCTb"])
    for nm, W1 in (("w1k", W1K), ("w1v", W1V)):
        src = D[nm].rearrange("(l d) n -> d l n", d=64)
        for half in range(2):
            for lq in range(4):
                P.dma("gpsimd", W1[half * 64:(half + 1) * 64, lq * 8:(lq + 1) * 8, :], src[:, lq * 8:(lq + 1) * 8, :], [], [nm])
    w2k_v = D["w2k"].rearrange("(c p) n -> p c n", p=128)
    P.dma("gpsimd", W2K2[:, :, 0:64], w2k_v, [], ["W2K2"])
    P.dma("gpsimd", W2K2[:, :, 64:128], w2k_v, [], ["W2K2"])
    P.dma("gpsimd", W2V[:, :, :], D["w2v"].rearrange("(c p) n -> p c n", p=128), [], ["W2V"])
    P.dma("sync", pef[0:64, 0, :], D["pek"].rearrange("l d -> d l"), [], ["pef"], allow_slow_non_contiguous=True)
    P.dma("sync", pef[0:64, 1, :], D["pev"].rearrange("l d -> d l"), [], ["pef"], allow_slow_non_contiguous=True)
    P.op("vector", I.tensor_copy(peb[0:64], pef[0:64]), ["pef"], ["peb"])
    P.op("vector", I.memset(a1T[:, :, :, :], 0.0), [], ["a1T"])
    for kv, (W1, nm) in enumerate(((W1K, "w1k"), (W1V, "w1v"))):
        for hc in range(2):
            for l in range(32):
                P.mm(bankv(2)[:, kv * 2 + hc:kv * 2 + hc + 1], W1[0:64, l, hc * 128:(hc + 1) * 128], peb[0:64, kv, l:l + 1],
                     l == 0, l == 31, [nm, "peb"], ["B2"])
    P.op("vector", I.tensor_copy(peW, bankv(2)[:, 0:4].rearrange("p (a b) -> p a b", a=2)), ["B2"], ["peW"])
    nbk = 0
    for kv, (W1, nm, SRC, skey) in enumerate(((W1K, "w1k", KCTb, "KCTb"), (W1V, "w1v", VCTb, "VCTb"))):
        for hc in range(2):
            bks = (3 + (nbk % 4), 3 + ((nbk + 1) % 4))
            nbk += 2
            for l in range(32):
                for g in range(2):
                    r0 = g * 64
                    P.mm(bankv(bks[g])[:, 0:255], W1[r0:r0 + 64, l, hc * 128:(hc + 1) * 128],
                         SRC[r0:r0 + 64, l % 16, l // 16:l // 16 + 255], l == 0, l == 31, [nm, skey], [f"B{bks[g]}"])
            for g in range(2):
                P.op("scalar", I.activation(
                    a1T[:, kv * 2 + g, hc, 0:255], bankv(bks[g])[:, 0:255], AF.Silu, bias=peW[:, kv, hc:hc + 1]),
                    [f"B{bks[g]}", "peW", "a1T"], ["a1T"])
    for g in range(2):
        for hc in range(2):
            P.mm(bankv(0)[:, 0:255], W2K2[:, hc, :], a1T[:, g, hc, 0:255], hc == 0, hc == 1, ["W2K2", "a1T"], ["B0"])
        P.op("vector", I.tensor_copy(KCc[g * 64:(g + 1) * 64, 0:255], bankv(0)[g * 64:(g + 1) * 64, 0:255]),
             ["B0", "KCc"], ["KCc"])
        for ch in range(2):
            for hc in range(2):
                P.mm(bankv(1)[:, 0:64], a1T[:, 2 + g, hc, ch * 128:(ch + 1) * 128], W2V[:, hc, :], hc == 0, hc == 1,
                     ["W2V", "a1T"], ["B1"])
            P.op("vector", I.tensor_copy(VCA[:, ch, g, 0:64], bankv(1)[:, 0:64]), ["B1", "VCA"], ["VCA"])

    dump("KCc", KCc, [128, 256], ["KCc"], BF16)
    dump("VCA", VCA, [128, 2, 2, 65], ["VCA"], BF16)
    phase_end("C")
    AE = Alloc(R0, ARENA_BYTES)
    P.barrier()
    WQ = AE.get([8, 1048], BF16)
    ot_lo = AE.cur
    OTF = AE.get([4, 2048], BF16)
    OTN = AE.get([4, 2048], BF16)
    ebw_lo = AE.cur
    EBW = AE.get([6, 8, 128], BF16)
    EBS = AE.get([3, 8, 128], BF16)
    AT_ = Alloc(AE.cur, ARENA_BYTES)
    tblf = AT_.get([8], F32)
    tbld = AT_.get([8], F32)
    t31r = AT_.get([8], F32)
    ones32 = AT_.get([128], F32)
    tblreps = [AT_.get([128], F32) for _ in range(2)]
    AT2 = Alloc(ot_lo, ot_lo + 32768)
    ohw = AT2.get([1023], F32)
    ohs = AT2.get([511], F32)
    ohc = AT2.get([3200], F32)
    wmask = AT_.get([1023], F32)
    smask = AT_.get([511], F32)
    cmk = AT_.get([25], F32)
    ebrows = [AT_.get([1024], F32) for _ in range(2)]
    ebrowbs = [AT_.get([1024], BF16) for _ in range(2)]
    P.dma("sync", tblf[0:32, :], D["tbl"][:, :], [], ["tblf"])
    P.dma("sync", t31r[0:32, :], D["tbl"][31:32, :].partition_broadcast(32)[:, 0, :], [], ["t31r"])
    P.dma("sync", T31, D["tbl"][31:32, :].partition_broadcast(128)[:, 0, :], [], ["T31"])
    P.dma("sync", ohw[0:32, :], D["ohw"][:, :], [], ["ohw"])
    P.dma("sync", ohs[0:32, :], D["ohs"][:, :], [], ["ohs"])
    P.dma("sync", ohc[0:32, :], D["ohc"][:, :], [], ["ohc"])
    P.dma("sync", wmask, D["wmask"][:, :], [], ["wmask"])
    P.dma("sync", smask, D["smask"][:, :], [], ["smask"])
    P.dma("sync", cmk, D["cm"][:, :], [], ["cmk"])
    P.op("vector", I.memset(ones32[0:32, :], 1.0), [], ["ones32"])
    P.op("vector", I.tensor_tensor(tbld[0:32, :], tblf[0:32, :], t31r[0:32, :], ALU.subtract), ["tblf", "t31r"], ["tbld"])
    for k in range(25):
        P.mm(bankv(0)[:, k * 8:(k + 1) * 8], ohc[0:32, k * 128:(k + 1) * 128], tblf[0:32, :], True, True, ["ohc", "tblf"], ["B0"])
    P.op("vector", I.tensor_tensor(BA.rearrange("p h k -> p k h"), bankv(0)[:, 0:200].rearrange("p (k h) -> p k h", k=25),
                                             cmk.rearrange("p (k o) -> p k o", o=1).broadcast_to([128, 25, 8]), ALU.add),
         ["B0", "cmk"], ["BA"])
    ebsw_ap = ebs_w.ap()
    ebss_ap = ebs_s.ap()
    for h in range(8):
        tblrep, ebrow, ebrowb = tblreps[0], ebrows[0], ebrowbs[0]
        P.op("vector", I.tensor_scalar(tblrep[0:32, :], ones32[0:32, :], tblf[0:32, h:h + 1], None, ALU.mult),
             ["ones32", "tblf"], ["tblrep0"])
        P.mm(bankv(1)[:, 0:512], tblrep[0:32, :], ohw[0:32, 0:512], True, True, ["tblrep0", "ohw"], ["B1"])
        P.mm(bankv(2)[:, 0:511], tblrep[0:32, :], ohw[0:32, 512:1023], True, True, ["tblrep0", "ohw"], ["B2"])
        P.op("scalar", I.activation(ebrow[:, 0:512], bankv(1)[:, 0:512], AF.Exp), ["B1"], ["ebrow0"])
        P.op("scalar", I.activation(ebrow[:, 512:1023], bankv(2)[:, 0:511], AF.Exp), ["B2", "ebrow0"], ["ebrow0"])
        P.op("vector", I.tensor_tensor(ebrowb[:, 0:1023], ebrow[:, 0:1023], wmask, ALU.mult), ["ebrow0", "wmask"], ["ebrowb0"])
        P.dma("sync", ebsw_ap[h * 128:(h + 1) * 128, :], ebrowb[:, 0:1023], ["ebrowb0"], [("ebs_w", h)])
        for k in range(6):
            src = bass.AP(ebs_w, h * 128 * 1023 + (4 - k) * 128 + 255, [[1022, 128], [1, 128]])
            P.dma("sync", EBW[:, k, h, :], src, [("ebs_w", h)], ["EBW"])
        tblrep, ebrow, ebrowb = tblreps[1], ebrows[1], ebrowbs[1]
        P.op("vector", I.tensor_scalar(tblrep[0:32, :], ones32[0:32, :], tbld[0:32, h:h + 1], None, ALU.mult),
             ["ones32", "tbld"], ["tblrep1"])
        P.mm(bankv(3)[:, 0:511], tblrep[0:32, :], ohs[0:32, 0:511], True, True, ["tblrep1", "ohs"], ["B3"])
        P.op("scalar", I.activation(ebrow[:, 0:511], bankv(3)[:, 0:511], AF.Exp), ["B3"], ["ebrow1"])
        P.op("vector", I.tensor_tensor(ebrowb[:, 0:511], ebrow[:, 0:511], smask, ALU.mult), ["ebrow1", "smask"], ["ebrowb1"])
        P.dma("sync", ebss_ap[h * 128:(h + 1) * 128, :], ebrowb[:, 0:511], ["ebrowb1"], [("ebs_s", h)])
        for k in range(3):
            src = bass.AP(ebs_s, h * 128 * 511 + (1 - k) * 128 + 255, [[510, 128], [1, 128]])
            P.dma("sync", EBS[:, k, h, :], src, [("ebs_s", h)], ["EBS"])

    wq_v = D["wq"].rearrange("(c p) n -> p c n", p=128)
    for d in range(8):
        P.dma("gpsimd", WQ[:, d, :], wq_v[:, d, :], [], ["WQ"])
    dump("EBW", EBW, [128, 6, 8, 128], ["EBW"], BF16)
    dump("EBS", EBS, [128, 3, 8, 128], ["EBS"], BF16)
    dump("BA", BA, [128, 8, 25], ["BA"])
    phase_end("T")
    P.barrier()

    AS = Alloc(AE.cur, ARENA_BYTES)
    xto = [AS.get([1024], F32)] * 2
    sqo = None
    xno = AS.get([1024], BF16)
    sso = AS.get([1], F32)
    rso = AS.get([1], F32)
    hTo = AS.get([8, 128], BF16)
    QF = AS.get([4, 256], BF16)
    QA = AS.get([8, 128], BF16)
    sigs = [AS.get([24], F32) for _ in range(2)]
    wgt = AS.get([32, 8], F32)
    wgb = AS.get([32, 8], BF16)
    imp2 = wgt[:, 0:8, :].rearrange("p a b -> p (a b)")
    Pts = [AS.get([512], BF16) for _ in range(5)]
    rl = AS.get([1, 8], F32)
    coef = AS.get([1, 8], F32)
    ofb = AS.get([8, 64], BF16)
    onf = AS.get([8, 64], F32)
    onb = AS.get([8, 64], BF16)
    sb = hTo.rearrange("p d t -> p (d t)").bitcast(F32).rearrange("p (a b) -> p a b", a=2)
    ebf = AS.get([4, 256], BF16)
    lsum = AS.get([8], F32)
    rlc = AS.get([8], F32)
    accp = AS.get([264], F32)
    imp = AS.get([64], F32)
    m8 = AS.get([16], F32)
    PN = AS.get([128], BF16)
    vma = [AS.get([128], F32)] * 2
    eT = xno.rearrange("p (r c t) -> p r c t", r=4, c=2)
    VPs = [AS.get([8, 65], BF16) for _ in range(4)]
    P.op("vector", I.memset(QF[:, :, :], 0.0), [], ["QF"])
    P.op("vector", I.memset(ebf[:, :, :], 0.0), [], ["ebf"])
    P.op("vector", I.memset(accp, 0.0), [], ["accp"])

    xo_v = D["xo"].rearrange("(j p) n -> j p n", p=128)
    sbank_ctr = [0]

    def sbank(pool=(0, 1, 2)):
        b = pool[sbank_ctr[0] % len(pool)]
        sbank_ctr[0] += 1
        return b

    pt_ctr = [0]

    def next_pt():
        i = pt_ctr[0] % 5
        pt_ctr[0] += 1
        return i

    def kfkey(j):
        return ("KF", j // 4)

    def prologue_a(g):
        norm_transpose(xo_v[g], xto[0], sqo, sso, rso, xno, hTo[:, :, :], gat, "E0", 7, [], "hTo", dma=(g == 0), on_dve=True)
        yield
        bq = 7
        for c in range(4):
            for d in range(8):
                P.mm(bankv(bq)[:, c * 128:(c + 1) * 128], WQ[:, d, c * 128:(c + 1) * 128], hTo[:, d, :], d == 0, d == 7,
                     ["WQ", "hTo"], [f"B{bq}"])
            yield
        P.op("vector", I.tensor_copy(QF[:, :, 0:128], bankv(bq)[:, 0:512].rearrange("p (c t) -> p c t", c=4)), [f"B{bq}", "QF"], ["QF"])

    def decay_weights(g):
        nj_ = 2 * g + 2
        P.op("vector", I.tensor_tensor(wgt[:, 0:nj_, :], NCt[:, 0:nj_, :], NCoff[:, 2 * g:2 * g + 1, :].broadcast_to([128, nj_, 8]), ALU.subtract),
             ["NCt", "NCoff", "wgt"], ["wgt"])
        P.op("scalar", I.activation(wgb[:, 0:nj_, :], wgt[:, 0:nj_, :], AF.Exp), ["wgt", "wgb"], ["wgb"])

    def prologue_b(g):
        sig = sigs[g % 2]
        skey = f"sig{g % 2}"
        bq2 = sbank()
        for c in range(4):
            for d in range(8):
                P.mm(bankv(bq2)[:, c * 128:(c + 1) * 128], WQ[:, d, 512 + c * 128:512 + (c + 1) * 128], hTo[:, d, :], d == 0, d == 7,
                     ["WQ", "hTo"], [f"B{bq2}"])
        P.op("vector", I.tensor_copy(QA[0:64, 0:4, :], bankv(bq2)[0:64, 0:512].rearrange("p (c t) -> p c t", c=4)),
             [f"B{bq2}"], ["QAq"])
        P.op("vector", I.tensor_copy(QA[64:128, 4:8, :], bankv(bq2)[64:128, 0:512].rearrange("p (c t) -> p c t", c=4)),
             [f"B{bq2}"], ["QAq"])
        bq3 = sbank()
        for d in range(8):
            P.mm(bankv(bq3)[:, 0:24], hTo[:, d, :], WQ[:, d, 1024:1048], d == 0, d == 7, ["WQ", "hTo"], [f"B{bq3}"])
        P.op("scalar", I.activation(sig, bankv(bq3)[:, 0:24], AF.Exp, scale=-1.0), [f"B{bq3}"], [skey])
        P.op("vector", I.tensor_scalar(sig, sig, 1.0, None, ALU.add), [skey], [skey])
        P.op("vector", I.reciprocal(sig, sig), [skey], [skey])

    for _ in prologue_a(0):
        pass
    prologue_b(0)
    decay_weights(0)

    pending_tail = [None]
    for g in range(nslots):
        for _ in range(3 if nslots == 16 else len(conv_jobs)):
            if conv_jobs:
                dv_, sv_, key_ = conv_jobs.pop(0)
                P.dma("gpsimd", dv_, sv_, [], [key_])
        sig = sigs[g % 2]
        skey = f"sig{g % 2}"
        sig3 = sig.rearrange("p (h b) -> p h b", b=3)

        def fin(obanks, br):
            for n_, ob in enumerate(obanks):
                ov = bankv(ob)[:, 0:260].rearrange("p (h d) -> p h d", h=4)
                P.op("vector", I.tensor_scalar(rl[:, 0, n_ * 4:(n_ + 1) * 4], ov[:, :, 64], 1e-30, None, ALU.max), [f"B{ob}", "rl"], ["rl"])
                P.op("vector", I.reciprocal(rl[:, 0, n_ * 4:(n_ + 1) * 4], rl[:, 0, n_ * 4:(n_ + 1) * 4]), ["rl"], ["rl"])
            if br is None:
                for n_, ob in enumerate(obanks):
                    ov = bankv(ob)[:, 0:260].rearrange("p (h d) -> p h d", h=4)
                    cv = rl[:, 0, n_ * 4:(n_ + 1) * 4].rearrange("p (h o) -> p h o", o=1).broadcast_to([128, 4, 64])
                    P.op("vector", I.tensor_tensor(ofb[:, n_ * 4:(n_ + 1) * 4, :], ov[:, :, 0:64], cv, ALU.mult),
                         [f"B{ob}", "rl", "ofb"], ["ofb"])
                return
            P.op("vector", I.tensor_tensor(coef[:, 0, :], rl[:, 0, :], sig3[:, :, br], ALU.mult), ["rl", skey], ["coef"])
            for n_, ob in enumerate(obanks):
                ov = bankv(ob)[:, 0:260].rearrange("p (h d) -> p h d", h=4)
                if br == 0:
                    cv = coef[:, 0, n_ * 4:(n_ + 1) * 4].rearrange("p (h o) -> p h o", o=1).broadcast_to([128, 4, 64])
                    P.op("vector", I.tensor_tensor(onf[:, n_ * 4:(n_ + 1) * 4, :], ov[:, :, 0:64], cv, ALU.mult),
                         [f"B{ob}", "coef", "onf"], ["onf"])
                else:
                    for hh in range(4):
                        h = n_ * 4 + hh
                        P.op("vector", I.scalar_tensor_tensor(onf[:, h, :], ov[:, hh, 0:64], coef[:, 0, h:h + 1], onf[:, h, :], ALU.mult, ALU.add),
                             [f"B{ob}", "coef", "onf"], ["onf"])

        phase_end("Eq")
        ncol = min(255, 16 * g + 15)
        c_lo = max(0, 16 * g - 10)
        k_lo = c_lo - (16 * g - 10)
        nk = ncol - c_lo
        nch = 1 if ncol <= 128 else 2
        nj = 2 * g + 2

        def cmp_gen(g=g, ncol=ncol, c_lo=c_lo, k_lo=k_lo, nk=nk, nch=nch, tail=None):
            hp = 4 if ncol <= 128 else 2
            cs = 512 // hp
            sbA_ = sb.rearrange("p a b -> p (a b)").rearrange("p (a b) -> p a b", a=hp)
            sbB_ = xto[0][:, 0:512].rearrange("p (a b) -> p a b", a=hp)
            bufs = ((sbA_, "hTo"), (sbB_, "E0xt"))
            P.dma("sync", vma[0], D["vmadd"][g], [], ["vma"])

            def heads_of(p):
                if hp == 4:
                    return p * 4, [p * 4 + r for r in range(4)]
                return (p // 2) * 4, [(p // 2) * 4 + (p % 2) * 2 + r for r in range(2)]

            def A(p):
                gbase, hs = heads_of(p)
                r0 = (gbase // 4) * 64
                for rr, h in enumerate(hs):
                    P.mm(bankv(7)[:, rr * cs:rr * cs + ncol], QA[r0:r0 + 64, h, :], KCc[r0:r0 + 64, 0:ncol], True, True,
                         ["QAq", "KCc"], ["B7"])

            def B(p):
                gbase, hs = heads_of(p)
                sbp, sk = bufs[p % 2]
                for rr, h in enumerate(hs):
                    col0 = rr * cs
                    if c_lo > 0:
                        P.op("vector", I.tensor_scalar(sbp[:, rr, 0:c_lo], bankv(7)[:, col0:col0 + c_lo], SCALE, T31[:, h:h + 1], ALU.mult, ALU.add),
                             ["B7", "T31", sk], [sk])
                    P.op("vector", I.scalar_tensor_tensor(sbp[:, rr, c_lo:ncol], bankv(7)[:, col0 + c_lo:col0 + ncol], SCALE,
                                                          BA[:, h, k_lo:k_lo + nk], ALU.mult, ALU.add), ["B7", "BA", sk], [sk])

            def C(p):
                gbase, hs = heads_of(p)
                sbp, sk = bufs[p % 2]
                for rr, h in enumerate(hs):
                    P.op("scalar", I.activation(sbp[:, rr, 0:ncol], sbp[:, rr, 0:ncol], AF.Exp, accum_out=lsum[:, h:h + 1]), [sk], [sk, "lsum"])

            def Dd(p):
                gbase, hs = heads_of(p)
                sbp, sk = bufs[p % 2]
                h0 = hs[0]
                r_lo = h0 - gbase
                P.op("gpsimd", I.tensor_copy(ebf[:, r_lo:r_lo + len(hs), 0:ncol], sbp[:, :, 0:ncol]), [sk, "ebf"], ["ebf"])
                P.op("vector", I.tensor_scalar(rlc[:, h0:h0 + len(hs)], lsum[:, h0:h0 + len(hs)], 1e-30, None, ALU.max), ["lsum", "rlc"], ["rlc"])
                P.op("vector", I.reciprocal(rlc[:, h0:h0 + len(hs)], rlc[:, h0:h0 + len(hs)]), ["rlc"], ["rlc"])
                for rr, h in enumerate(hs):
                    if h == gbase:
                        P.op("vector", I.tensor_scalar(accp[:, 1:1 + ncol], sbp[:, rr, 0:ncol], rlc[:, h:h + 1], None, ALU.mult), [sk, "rlc", "accp"], ["accp"])
                    else:
                        P.op("vector", I.scalar_tensor_tensor(accp[:, 1:1 + ncol], sbp[:, rr, 0:ncol], rlc[:, h:h + 1], accp[:, 1:1 + ncol], ALU.mult, ALU.add),
                             [sk, "rlc", "accp"], ["accp"])

            def E(grp):
                P.op("vector", I.tensor_reduce(imp, accp[:, 1:257].rearrange("p (j f) -> p j f", f=4), AX.X, ALU.add), ["accp"], ["imp"])
                P.op("vector", I.tensor_tensor(imp, imp, accp[:, 0:256:4], ALU.add), ["imp", "accp"], ["imp"])
                P.op("vector", I.tensor_tensor(imp, imp, vma[0][:, 0:64], ALU.mult), ["imp", "vma"], ["imp"])
                P.op("vector", I.tensor_tensor(imp, imp, vma[0][:, 64:128], ALU.add), ["imp", "vma"], ["imp"])

            def F(grp):
                P.op("vector", I.max(m8[:, 0:8], imp), ["imp"], ["m8"])
                P.op("vector", I.match_replace(imp2, m8[:, 0:8], imp, -1e30), ["imp", "m8", "wgt"], ["wgt"])
                P.op("vector", I.max(m8[:, 8:16], imp2), ["wgt", "m8"], ["m8"])
                pc0 = 64 if grp == 0 else 0
                P.op("vector", I.tensor_scalar(PN[:, pc0:pc0 + 64], imp, m8[:, 15:16], 1.0, ALU.is_ge, ALU.subtract), ["imp", "m8", "PN"], ["PN"])

            def G(grp):
                for r in range(4):
                    for ch in range(nch):
                        P.tr(bankv(7, BF16)[:, (r * 2 + ch) * 128:(r * 2 + ch + 1) * 128], ebf[:, r, ch * 128:(ch + 1) * 128], identb,
                             ["ebf", "identb"], ["B7"])

            def H(grp):
                P.op("vector", I.tensor_copy(eT[:, :, 0:nch, :],
                                             bankv(7, BF16)[:, 0:1024].rearrange("p (r c t) -> p r c t", r=4, c=2)[:, :, 0:nch, :]), ["B7", "E0xn"], ["E0xn"])

            def Ii(grp):
                for r in range(4):
                    h = grp * 4 + r
                    for ch in range(nch):
                        P.mm(bankv(6)[:, h * 64:h * 64 + 64], eT[:, r, ch, :], VCA[:, ch, grp, 0:64], (h == 0 and ch == 0), False,
                             ["E0xn", "VCA"], ["B6"])

            def J(_):
                P.op("vector", I.tensor_tensor(coef[:, 0, :], rlc, sig3[:, :, 0], ALU.mult), ["rlc", skey], ["coef"])
                P.op("vector", I.tensor_tensor(onf, bankv(6)[:, 0:512].rearrange("p (h d) -> p h d", h=8),
                                               coef[:, 0, :].rearrange("p (h o) -> p h o", o=1).broadcast_to([128, 8, 64]), ALU.mult),
                     ["B6", "coef", "onf"], ["onf"])

            def K(_):
                P.tr(bankv(7, BF16)[:, 0:128], PN, identb, ["PN", "identb"], ["B7"])

            def L(_):
                P.op("vector", I.tensor_copy(QA[64:128, 0:4, :], bankv(7, BF16)[64:128, 0:128].rearrange("p (o t) -> p o t", o=1).broadcast_to([64, 4, 128])),
                     ["B7"], ["QAp"])
                P.op("vector", I.tensor_copy(QA[0:64, 4:8, :], bankv(7, BF16)[0:64, 0:128].rearrange("p (o t) -> p o t", o=1).broadcast_to([64, 4, 128])),
                     ["B7"], ["QAp"])

            if hp == 2:
                order = [(A, 0), (B, 0), (A, 1), (C, 0), (B, 1), (A, 2), (Dd, 0), (C, 1), (B, 2), (A, 3), (Dd, 1), (C, 2), (B, 3),
                         (G, 0), (E, 0), (H, 0), (Dd, 2), (F, 0), (C, 3), (Ii, 0), (Dd, 3), (E, 1), (F, 1), (G, 1), (H, 1), (Ii, 1),
                         (J, 0), (K, 0), (L, 0)]
            else:
                order = [(A, 0), (B, 0), (A, 1), (C, 0), (B, 1), (Dd, 0), (G, 0), (E, 0), (C, 1), (H, 0), (F, 0), (Dd, 1), (Ii, 0),
                         (E, 1), (F, 1), (G, 1), (H, 1), (Ii, 1), (J, 0), (K, 0), (L, 0)]
            splice_at = max(i_ for i_, (f_, _a) in enumerate(order) if f_ is B)
            for n_, (fn_, arg_) in enumerate(order):
                fn_(arg_)
                yield
                if fn_ in (A, K) or n_ >= len(order) - 8:
                    yield
                if n_ == splice_at and tail is not None:
                    yield from tail


        pend = []
        DEPTH = 4

        def push(fn):
            pend.append(fn)
            while len(pend) > DEPTH:
                pend.pop(0)()

        def flush():
            while pend:
                pend.pop(0)()

        def attend(tiles, batches, obanks, post, filler=None, fin_args=None, stride=1, drain=True, spool=(0, 1, 2), after_qk=None, interleave_rows=False, burst=1, spool2=None, spool2_from=0):
            first = {b: True for b in obanks}
            gi = 0
            for ti, tile in enumerate(tiles):
                if "pre" in tile:
                    tile["pre"]()
                pre_banks = None
                if interleave_rows:
                    pre_banks = [sbank(spool) for _ in batches]
                    if "qkg" in tile:
                        for bi, heads_ in enumerate(batches):
                            for (c0, ncols, lhsT, rhs, rk) in tile["qkg"](heads_):
                                P.mm(bankv(pre_banks[bi])[:, c0:c0 + ncols], lhsT, rhs, True, True, rk, [f"B{pre_banks[bi]}"])
                    else:
                        for hh in range(4):
                            for bi, heads_ in enumerate(batches):
                                lhsT, rhs, rk = tile["qk"](heads_[hh])
                                P.mm(bankv(pre_banks[bi])[:, hh * 128:(hh + 1) * 128], lhsT, rhs, True, True, rk, [f"B{pre_banks[bi]}"])
                for bi, heads in enumerate(batches):
                    if pre_banks is not None:
                        bk = pre_banks[bi]
                    else:
                        bk = sbank(spool2 if (spool2 is not None and gi >= spool2_from) else spool)
                        if "qkg" in tile:
                            for (c0, ncols, lhsT, rhs, rk) in tile["qkg"](heads):
                                P.mm(bankv(bk)[:, c0:c0 + ncols], lhsT, rhs, True, True, rk, [f"B{bk}"])
                        else:
                            for hh, h in enumerate(heads):
                                lhsT, rhs, rk = tile["qk"](h)
                                P.mm(bankv(bk)[:, hh * 128:(hh + 1) * 128], lhsT, rhs, True, True, rk, [f"B{bk}"])
                    pi = next_pt()
                    Pt = Pts[pi]
                    pk = f"Pt{pi}"
                    P.op("scalar", I.activation(Pt, bankv(bk)[:, 0:512], AF.Exp, scale=SCALE), [f"B{bk}"], [pk])
                    post(ti, tile, heads, Pt, pk)

                    def pv(tile=tile, heads=heads, Pt=Pt, pk=pk):
                        for hh, h in enumerate(heads):
                            ob = obanks[h // 4]
                            rhsv, vk = tile["v"](h)
                            P.mm(bankv(ob)[:, (h % 4) * 65:(h % 4) * 65 + 65], Pt[:, hh * 128:(hh + 1) * 128], rhsv, first[ob], False,
                                 [pk] + vk, [f"B{ob}"])
                            first[ob] = False
                    push(pv)
                    gi += 1
                    if filler is not None and gi % stride == 0:
                        for _ in range(burst):
                            next(filler, None)
            if filler is not None and drain:
                for _ in filler:
                    pass
            if after_qk is not None:
                after_qk()
            push(lambda: fin(*fin_args))

        fox_tiles = []
        for j in range(nj):
            def qk(h, j=j):
                r0 = (h % 2) * 64
                return (KF[r0:r0 + 64, h // 2, j * 128:(j + 1) * 128], QF[r0:r0 + 64, h // 2, 0:128], [kfkey(j), "QF"])

            def vv(h, j=j):
                return (VPs[j % 4][:, h, :], [f"VP{j % 4}"])

            def pre(j=j):
                wv = wgb[:, j, :].rearrange("p (h o) -> p h o", o=1).broadcast_to([128, 8, 65])
                P.op("vector" if j % 2 == 0 else "gpsimd", I.tensor_tensor(VPs[j % 4], VF[:, j, :, :], wv, ALU.mult),
                     [("VF", j), "VFone", "wgb", f"VP{j % 4}"], [f"VP{j % 4}"])
            fox_tiles.append(dict(j=j, qk=qk, v=vv, pre=pre))

        def fox_post(ti, tile, heads, Pt, pk):
            j = tile["j"]
            if j >= nj - 2:
                P3 = Pt.rearrange("p (h t) -> p h t", h=4)
                mk = m0b if j == nj - 2 else m1b
                mv = mk.rearrange("p (o t) -> p o t", o=1).broadcast_to([128, 4, 128])
                P.op("vector", I.tensor_tensor(P3, P3, mv, ALU.mult), [pk, "m0b", "m1b"], [pk])

        def chain_gen(*gens):
            for gen_ in gens:
                if gen_ is not None:
                    yield from gen_

        cmpf = cmp_gen(tail=pending_tail[0])
        pending_tail[0] = None
        n_stage = 44
        n_groups = 2 * nj + 2 * min(6, 2 * g + 2)
        cstride = max(1, n_groups // n_stage)
        cburst = max(1, -(-n_stage // n_groups))
        attend(fox_tiles, [[0, 2, 4, 6], [1, 3, 5, 7]], [3, 4], fox_post, filler=cmpf, fin_args=([3, 4], None), stride=cstride, drain=False, spool=(0, 1, 2, 5),
               interleave_rows=True, burst=cburst)
        phase_end("Efox")

        win_tiles = []
        for k in range(6):
            j = 2 * g - 4 + k
            if j < 0:
                continue

            def qkg(heads, j=j):
                gi_ = heads[0] // 4
                r0 = gi_ * 64
                return [(0, 512, KW[r0:r0 + 64, j * 128:(j + 1) * 128],
                         QA[r0:r0 + 64, 4 * gi_:4 * gi_ + 4, :].rearrange("p h t -> p (h t)"), [("KW", j // 4), "QAq"])]

            def vv(h, j=j):
                return (VW[:, j, h // 4, :], [("VW", j), "VWone"])
            win_tiles.append(dict(j=j, k=k, qkg=qkg, v=vv))

        def win_post(ti, tile, heads, Pt, pk):
            h0 = heads[0]
            k = tile["k"]
            P3 = Pt.rearrange("p (h t) -> p h t", h=4)
            P.op("vector", I.tensor_tensor(P3, P3, EBW[:, k, h0:h0 + 4, :], ALU.mult), [pk, "EBW"], [pk])

        attend(win_tiles, [[0, 1, 2, 3], [4, 5, 6, 7]], [3, 4], win_post, filler=cmpf, fin_args=([3, 4], 2), stride=cstride, drain=True, spool=(0, 1, 2, 5), interleave_rows=True, burst=cburst)
        if g + 1 < nslots:
            P.dma("sync", xto[0], xo_v[g + 1], [], ["E0xt"])
        slc_tiles = []
        for j in range(nj):
            def qkg(heads, j=j):
                gi_ = heads[0] // 4
                KSg = KS0 if gi_ == 0 else KS1
                return [(0, 512, KSg[:, j * 128:(j + 1) * 128], QA[:, 4 * gi_:4 * gi_ + 4, :].rearrange("p h t -> p (h t)"),
                         [("KS0", j // 4), ("KS1", j // 4), "KS0i", "KS1i", "QAq", "QAp"])]

            def vv(h, j=j):
                return (VS[:, j, h // 4, :], [("VS", j), "VSone"])
            slc_tiles.append(dict(j=j, qkg=qkg, v=vv))

        def slc_post(ti, tile, heads, Pt, pk):
            j = tile["j"]
            kk = j - (2 * g - 1)
            if kk >= 0:
                h0 = heads[0]
                P3 = Pt.rearrange("p (h t) -> p h t", h=4)
                P.op("vector", I.tensor_tensor(P3, P3, EBS[:, kk, h0:h0 + 4, :], ALU.mult), [pk, "EBS"], [pk])

        pf = prologue_a(g + 1) if g + 1 < nslots else None
        attend(slc_tiles, [[0, 1, 2, 3], [4, 5, 6, 7]], [5, 6], slc_post, filler=pf, fin_args=([5, 6], 1), stride=max(1, (2 * nj) // 6), drain=True, burst=max(1, -(-10 // (2 * nj))), spool=(0, 1, 2), spool2=(0, 1, 2, 3, 4), spool2_from=4,

               after_qk=(lambda: (decay_weights(g + 1), prologue_b(g + 1))) if g + 1 < nslots else None)
        phase_end("Eslc")

        flush()
        phase_end("Ewin")

        def tail_gen(g=g):
            P.op("vector", I.tensor_copy(onb, onf), ["onf"], ["onb"])
            yield
            yield
            ofb2 = ofb.rearrange("p h d -> p (h d)")
            onb2 = onb.rearrange("p h d -> p (h d)")
            for c in range(4):
                P.tr(bankv(7, BF16)[:, c * 128:(c + 1) * 128], ofb2[:, c * 128:(c + 1) * 128], identb, ["ofb", "identb"], ["B7"])
            for c in range(4):
                P.tr(bankv(7, BF16)[:, 512 + c * 128:512 + (c + 1) * 128], onb2[:, c * 128:(c + 1) * 128], identb, ["onb", "identb"], ["B7"])
            yield
            yield
            P.op("vector", I.tensor_copy(OTF[:, :, g * 128:(g + 1) * 128], bankv(7, BF16)[:, 0:512].rearrange("p (c t) -> p c t", c=4)),
                 ["B7"], ["OTF"])
            P.op("vector", I.tensor_copy(OTN[:, :, g * 128:(g + 1) * 128], bankv(7, BF16)[:, 512:1024].rearrange("p (c t) -> p c t", c=4)),
                 ["B7"], ["OTN"])
            yield

        pending_tail[0] = tail_gen()

    for _ in pending_tail[0]:
        pass

    dump("rlc", rlc, [128, 8], ["rlc"])
    dump("lsum", lsum, [128, 4], ["lsum"])
    dump("coef", coef, [128, 1, 8], ["coef"])
    dump("onf", onf, [128, 8, 64], ["onf"])
    phase_end("Efin")
    dump("OTF", OTF, [128, 4, 2048], ["OTF"], BF16)
    dump("OTN", OTN, [128, 4, 2048], ["OTN"], BF16)
    phase_end("E")
    P.barrier()
    AF1 = Alloc(misc_end, R0)
    WG = AF1.get([8, 2048], BF16)
    WBF = AF1.get([4, 1024], BF16)
    WBN = AF1.get([4, 1024], BF16)
    MT = Alloc(ebw_lo, ARENA_BYTES).get([8, 2048], BF16)
    hT4 = [AF1.get([8, 512], BF16) for _ in range(2)]
    xt1 = [AF1.get([1024], F32) for _ in range(2)]
    sq1 = None
    xn1 = [AF1.get([1024], BF16) for _ in range(2)]
    ss1 = [AF1.get([1], F32) for _ in range(2)]
    rs1 = [AF1.get([1], F32) for _ in range(2)]
    sga = AF1.get([512], F32)
    sgb = AF1.get([512], F32)
    t1 = AF1.get([512], F32)
    t2 = AF1.get([512], F32)
    f1_end = AF1.cur
    WO = AF1.get([8, 1024], BF16)
    gfr = AF1.get([1024], F32)
    wo_end = AF1.cur
    wg_v = CONV["wg"].rearrange("(c p) n -> p c n", p=128)
    for blk in range(2):
        P.dma("sync", WG[:, :, blk * 512:(blk + 1) * 512], wg_v[:, :, blk * 512:(blk + 1) * 512], ["wg_bf"], [f"WGa{blk}"])
        P.dma("sync", WG[:, :, 1024 + blk * 512:1024 + (blk + 1) * 512], wg_v[:, :, 1024 + blk * 512:1024 + (blk + 1) * 512], ["wg_bf"], [f"WGb{blk}"])
    P.dma("sync", WBF[:, :, :], CONV["wbf"].rearrange("(c p) n -> p c n", p=128), ["wbf_bf"], ["WBF"])
    P.dma("sync", WBN[:, :, :], CONV["wbn"].rearrange("(c p) n -> p c n", p=128), ["wbn_bf"], ["WBN"])
    P.dma("sync", WO[:, :, :], CONV["wo"].rearrange("(c p) n -> p c n", p=128), ["wo_bf"], ["WO"])
    P.dma("sync", gfr, D["gfin"].partition_broadcast(128)[:, 0, :], [], ["gfr"])
    def nt_f1(stl_, tts=(0, 1, 2, 3)):
        for tt in tts:
            j = stl_ * 4 + tt
            b = j % 2
            norm_transpose(xo_v[j], xt1[b], sq1, ss1[b], rs1[b], xn1[b], hT4[stl_ % 2][:, :, tt * 128:(tt + 1) * 128], gat, f"F{b}", 7, [],
                           f"hF{stl_ % 2}")

    nt_f1(0)
    for stl in range(4):
        hT = hT4[stl % 2]
        hkey = f"hF{stl % 2}"
        tsl = slice(stl * 512, (stl + 1) * 512)
        for c in range(8):
            for d in range(8):
                P.mm(bankv(0)[:, 0:512], WG[:, d, c * 128:(c + 1) * 128], hT[:, d, :], d == 0, d == 7, [f"WGa{c // 4}", hkey], ["B0"])
            for d in range(8):
                P.mm(bankv(1)[:, 0:512], WG[:, d, 1024 + c * 128:1024 + (c + 1) * 128], hT[:, d, :], d == 0, d == 7, [f"WGb{c // 4}", hkey], ["B1"])
            for d in range(4):
                P.mm(bankv(2)[:, 0:512], WBF[:, d, c * 128:(c + 1) * 128], OTF[:, d, tsl], d == 0, d == 3, ["WBF", "OTF"], ["B2"])
            for d in range(4):
                P.mm(bankv(3)[:, 0:512], WBN[:, d, c * 128:(c + 1) * 128], OTN[:, d, tsl], d == 0, d == 3, ["WBN", "OTN"], ["B3"])
            if stl + 1 < 4 and c < 4:
                nt_f1(stl + 1, (c,))
            P.op("scalar", I.activation(sga, bankv(0)[:, 0:512], AF.Sigmoid), ["B0"], ["sga"])
            P.op("scalar", I.activation(sgb, bankv(1)[:, 0:512], AF.Sigmoid), ["B1"], ["sgb"])
            P.op("vector", I.tensor_tensor(t1, bankv(2)[:, 0:512], sga, ALU.mult), ["B2", "sga"], ["t1"])
            P.op("vector", I.tensor_tensor(t2, bankv(3)[:, 0:512], sgb, ALU.mult), ["B3", "sgb"], ["t2"])
            P.op("gpsimd", I.tensor_tensor(MT[:, c, tsl], t1, t2, ALU.add), ["t1", "t2"], [("MT", stl)])

    dump("MT", MT, [128, 8, 2048], [("MT", i) for i in range(4)], BF16)
    phase_end("F1")
    P.barrier()
    AF2a = Alloc(misc_end, f1_end)
    AF2b = Alloc(wo_end, ebw_lo)
    ATt = AF2a.get([32, 512], BF16)
    x1s = AF2a.get([4, 1024], F32)
    FF1 = [AF2a.get([8, 512], BF16) for _ in range(3)]
    h2T = AF2a.get([8, 512], BF16)
    FF2 = [AF2b.get([4, 1024], BF16) for _ in range(3)]
    xt2 = [AF2b.get([1024], F32) for _ in range(2)]
    xn2 = [AF2b.get([1024], BF16) for _ in range(2)]
    ss2 = [AF2b.get([1], F32) for _ in range(2)]
    rs2 = [AF2b.get([1], F32) for _ in range(2)]
    sqr = [AF2b.get([512], F32) for _ in range(2)]
    oto = [AF2b.get([1024], F32) for _ in range(2)]
    wf1_v = CONV["wf1"].rearrange("(c p) n -> p c n", p=128)
    wf2_v = CONV["wf2"].rearrange("(c p) n -> p c n", p=128)
    ff1_ctr = [0]
    ff2_ctr = [0]
    for stl in range(4):
        tsl = slice(stl * 512, (stl + 1) * 512)
        for tt in range(4):
            j = stl * 4 + tt
            b = j % 2
            t0 = stl * 512 + tt * 128
            if not (stl > 0 and tt < 2):
                P.dma("sync", xt2[b], xo_v[j], [], [f"xt2{b}"])
            for half in range(2):
                bk = tt * 2 + half
                for d in range(8):
                    P.mm(bankv(bk)[:, 0:512], MT[:, d, t0:t0 + 128], WO[:, d, half * 512:(half + 1) * 512], d == 0, d == 7,
                         [("MT", stl), "WO"], [f"B{bk}"])
                P.op("vector", I.tensor_tensor(
                    x1s[:, tt, half * 512:(half + 1) * 512], bankv(bk)[:, 0:512], xt2[b][:, half * 512:(half + 1) * 512], ALU.add),
                    [f"B{bk}", f"xt2{b}"], [("x1", tt)])
        for tt in range(4):
            b = tt % 2
            P.op("scalar", I.activation(xn2[b], x1s[:, tt, :], AF.Square, accum_out=ss2[b]), [("x1", tt)], [f"xn2{b}", f"ss2{b}"])
            P.op("scalar", I.activation(rs2[b], ss2[b], AF.Ln, bias=EPS_AP, scale=1.0 / 1024), [f"ss2{b}", "epsap"], [f"rs2{b}"])
            P.op("scalar", I.activation(rs2[b], rs2[b], AF.Exp, scale=-0.5), [f"rs2{b}"], [f"rs2{b}"])
            P.op("gpsimd", I.tensor_scalar(xn2[b], x1s[:, tt, :], rs2[b], 1.0, ALU.mult, ALU.mult), [("x1", tt), f"rs2{b}"], [f"xn2{b}"])
            tb = tt % 2
            pb = bankv(tb, BF16)
            for d in range(8):
                P.tr(pb[:, d * 128:(d + 1) * 128], xn2[b][:, d * 128:(d + 1) * 128], identb, [f"xn2{b}", "identb"], [f"B{tb}"])
            P.op("vector", I.tensor_tensor(h2T[:, :, tt * 128:(tt + 1) * 128], pb.rearrange("p (d t) -> p d t", d=8),
                                           gml.rearrange("p (d o) -> p d o", o=1).broadcast_to([128, 8, 128]), ALU.mult),
                 [f"B{tb}", "gml"], ["h2T"])
        for fc in range(8):
            wb = ff1_ctr[0] % 3
            ff1_ctr[0] += 1
            for d in range(8):
                P.dma("sync", FF1[wb][:, d, :], wf1_v[:, d, fc * 512:(fc + 1) * 512], ["wf1_bf"], [f"FF1{wb}"])
            for q4 in range(4):
                ffc = fc * 4 + q4
                bk = 2 + (ffc % 4)
                for d in range(8):
                    P.mm(bankv(bk)[:, 0:512], FF1[wb][:, d, q4 * 128:(q4 + 1) * 128], h2T[:, d, :], d == 0, d == 7,
                         [f"FF1{wb}", "h2T"], [f"B{bk}"])
                sq_ = sqr[ffc % 2]
                P.op("scalar", I.activation(sq_, bankv(bk)[:, 0:512], AF.Square), [f"B{bk}"], [f"sqr{ffc % 2}"])
                P.op("vector", I.scalar_tensor_tensor(ATt[:, ffc, :], bankv(bk)[:, 0:512], 0.0, sq_, ALU.is_gt, ALU.mult),
                     [f"B{bk}", f"sqr{ffc % 2}"], ["ATt"])
        for fc in range(8):
            wb = ff2_ctr[0] % 3
            ff2_ctr[0] += 1
            for q4 in range(4):
                P.dma("sync", FF2[wb][:, q4, :], wf2_v[:, fc * 4 + q4, :], ["wf2_bf"], [f"FF2{wb}"])
            for tt in range(4):
                for q4 in range(4):
                    ffc = fc * 4 + q4
                    for half in range(2):
                        P.mm(bankv(tt * 2 + half)[:, 0:512], ATt[:, ffc, tt * 128:(tt + 1) * 128], FF2[wb][:, q4, half * 512:(half + 1) * 512],
                             ffc == 0, ffc == 31, ["ATt", f"FF2{wb}"], [f"B{tt * 2 + half}"])
        if stl + 1 < 4:
            for tt_ in range(2):
                P.dma("sync", xt2[tt_], xo_v[(stl + 1) * 4 + tt_], [], [f"xt2{tt_}"])
        for tt in range(4):
            j = stl * 4 + tt
            b = j % 2
            for half in range(2):
                P.op("vector", I.tensor_tensor(
                    x1s[:, tt, half * 512:(half + 1) * 512], bankv(tt * 2 + half)[:, 0:512], x1s[:, tt, half * 512:(half + 1) * 512], ALU.add),
                    [f"B{tt * 2 + half}", ("x1", tt)], [("x1", tt)])
            P.op("scalar", I.activation(oto[b], x1s[:, tt, :], AF.Square, accum_out=ss2[b]), [("x1", tt)], [f"oto{b}", f"ss2{b}"])
            P.op("scalar", I.activation(rs2[b], ss2[b], AF.Ln, bias=EPS_AP, scale=1.0 / 1024), [f"ss2{b}", "epsap"], [f"rs2{b}"])
            P.op("scalar", I.activation(rs2[b], rs2[b], AF.Exp, scale=-0.5), [f"rs2{b}"], [f"rs2{b}"])
            P.op("vector", I.scalar_tensor_tensor(oto[b], x1s[:, tt, :], rs2[b], gfr, ALU.mult, ALU.mult),
                 [("x1", tt), f"rs2{b}", "gfr"], [f"oto{b}"])
            P.dma("sync", out[j * 128:(j + 1) * 128, :], oto[b], [f"oto{b}"], [])

    return


_CACHE = {}


def _prep_inputs(inp, core):
    b, par = core // 2, core % 2
    f = lambda a: np.ascontiguousarray(np.asarray(a, dtype=np.float32))
    x = f(inp["x"])
    w_in = f(inp["w_in"])[0]
    o = np.cumsum([0, 512, 512, 512, 8, 512, 128, 128, 128, 128, 128, 128, 24, 1024, 1024])
    fq, fk, fv, flog, nq, kc, vc, ksl, vsl, kwn, vwn, ngate, ga, gb = [w_in[:, o[i]:o[i + 1]] for i in range(14)]
    nq_pairs = np.concatenate([np.concatenate([nq[:, h * 64:(h + 1) * 64], nq[:, (h + 4) * 64:(h + 5) * 64]], axis=1) for h in range(4)], axis=1)
    own = np.concatenate([np.arange((2 * g + par) * 128, (2 * g + par + 1) * 128) for g in range(16)])
    d = dict(
        xf=x[b], xo=x[b][own],
        wkv=np.concatenate([fk, kc, vc, ksl, kwn, fv, vsl, vwn, flog], axis=1),
        wq=np.concatenate([fq, nq_pairs, ngate], axis=1),
        wg=np.concatenate([ga, gb], axis=1),
        gat=f(inp["g_attn"])[0].reshape(8, 128).T, gml=f(inp["g_mlp"])[0].reshape(8, 128).T,
        gfin=f(inp["g_final"]).reshape(1, 1024), bfg=f(inp["b_forget"]).reshape(1, 8), tbl=f(inp["rel_bias_table"]),
        pek=f(inp["cmp_pe_k"])[0], pev=f(inp["cmp_pe_v"])[0], w1k=f(inp["cmp_w1_k"])[0], w1v=f(inp["cmp_w1_v"])[0],
        w2k=f(inp["cmp_w2_k"])[0], w2v=f(inp["cmp_w2_v"])[0],
        wbf=f(inp["w_br_fox"])[0], wbn=f(inp["w_br_nsa"])[0], wo=f(inp["w_out"])[0],
        wf1=f(inp["w_ff1"])[0], wf2=f(inp["w_ff2"])[0],
    )
    d.update(make_consts(par))
    return {k: np.ascontiguousarray(v, dtype=np.float32) for k, v in d.items()}, own


def kernel(**inputs):
    nc = bass.Bass("TRN2", target_bir_lowering=False)
    build(nc)
    in_maps, owns = [], []
    for core in range(8):
        m, own = _prep_inputs(inputs, core)
        in_maps.append(m)
        owns.append(own)
    res = run_bass_kernel_spmd(nc, in_maps, core_ids=list(range(8)))
    outp = np.zeros((4, 4096, 1024), np.float32)
    for core in range(8):
        outp[core // 2, owns[core]] = res.results[core]["out"]
    return outp
```

```python
import math
import contextlib
import numpy as np
import concourse.bass as bass
import concourse.mybir as mybir
from concourse.bass_utils import run_bass_kernel_spmd

F32 = mybir.dt.float32
BF16 = mybir.dt.bfloat16
ALU = mybir.AluOpType
AF = mybir.ActivationFunctionType
AX = mybir.AxisListType

ENGINES = ["tensor", "vector", "scalar", "gpsimd", "sync"]
NDMA_SEM = 8
SCALE = 0.125
EPS = 1e-6
BIG = 30000.0


class _Rec:
    def __getattr__(self, name):
        def rec(*a, **kw):
            return lambda e: getattr(e, name)(*a, **kw)
        return rec


I = _Rec()


class Prog:
    def __init__(self, nc):
        self.nc = nc
        self.instrs = []

    def op(self, eng, fn, reads=(), writes=(), dma=False):
        self.instrs.append(dict(eng=eng, fn=fn, reads=tuple(reads), writes=tuple(writes), dma=dma, bar=False))
        return len(self.instrs) - 1

    def barrier(self):
        self.instrs.append(dict(eng=None, fn=None, reads=(), writes=(), dma=False, bar=True))

    def mm(self, out, lhsT, rhs, start, stop, reads, writes):
        return self.op("tensor", I.matmul(out, lhsT, rhs, start=start, stop=stop,
                                                     skip_group_check=True), reads, writes)

    def tr(self, out, in_, ident, reads, writes):
        return self.op("tensor", I.transpose(out, in_, ident), reads, writes)

    def dma(self, eng, out, in_, reads, writes, **kw):
        return self.op(eng, I.dma_start(out=out, in_=in_, **kw), reads, writes, dma=True)

    def emit(self):
        nc = self.nc
        instrs = self.instrs
        n = len(instrs)
        last_writer, readers, group_deps = {}, {}, {}
        deps = [set() for _ in range(n)]
        has_dep = [False] * n
        last_on_eng = {}
        last_dmas = {}
        pending_bar = {}
        dma_count = {e: 0 for e in ENGINES}
        dma_key = [None] * n
        prev_same_dma = [None] * n
        last_on_dma_sem = {}
        for i, ins in enumerate(instrs):
            if ins["bar"]:
                allprev = set(last_on_eng.values()) | set(last_on_dma_sem.values())
                for e in ENGINES:
                    pending_bar[e] = set(allprev)
                continue
            e = ins["eng"]
            d = set()
            for r in ins["reads"]:
                d.update(last_writer.get(r, ()))
            joined = set()
            for r in ins["writes"]:
                lw = last_writer.get(r, [])
                if (ins["dma"] and lw and not readers.get(r) and all(instrs[w]["dma"] for w in lw)):
                    d.update(group_deps.get(r, ()))
                    joined.add(r)
                else:
                    gd = set(lw) | set(readers.get(r, {}).values())
                    d.update(gd)
                    group_deps[r] = gd
            for r in ins["reads"]:
                rk = ("dma", i) if ins["dma"] else e
                readers.setdefault(r, {})[rk] = i
            for r in ins["writes"]:
                if r in joined:
                    last_writer[r].append(i)
                else:
                    last_writer[r] = [i]
                    readers[r] = {}
            d.discard(i)
            if e == "tensor" and not ins["dma"]:
                d = {j for j in d if not (instrs[j]["eng"] == "tensor" and not instrs[j]["dma"])}
            if e in pending_bar:
                d.update(pending_bar.pop(e))
            if ins["dma"]:
                k = dma_count[e]
                dma_count[e] += 1
                key = ("dma", e, k % NDMA_SEM)
                dma_key[i] = (key, 16 * (k // NDMA_SEM + 1))
                prev_same_dma[i] = last_on_dma_sem.get(key)
                last_on_dma_sem[key] = i
            else:
                last_on_eng[e] = i
            deps[i] = d
            for j in d:
                has_dep[j] = True
        eng_count = {e: 0 for e in ENGINES}
        ticket = [None] * n
        for i, ins in enumerate(instrs):
            if ins["bar"]:
                continue
            if ins["dma"]:
                ticket[i] = dma_key[i]
            elif has_dep[i]:
                eng_count[ins["eng"]] += 1
                ticket[i] = (("eng", ins["eng"]), eng_count[ins["eng"]])
        self.stats = dict(n=n, eng_count=dict(eng_count), dma_count=dict(dma_count))
        with contextlib.ExitStack() as st:
            sems = {}
            for e in ENGINES:
                sems[("eng", e)] = st.enter_context(nc.semaphore(f"s_{e}"))
                for k in range(NDMA_SEM):
                    sems[("dma", e, k)] = st.enter_context(nc.semaphore(f"d_{e}_{k}"))
            block = st.enter_context(nc.Block())
            per_eng = {e: [i for i in range(n) if instrs[i]["eng"] == e] for e in ENGINES}
            final_vals = {}
            for i in range(n):
                if ticket[i] is not None:
                    key, v = ticket[i]
                    final_vals[key] = max(final_vals.get(key, 0), v)

            def make(e):
                def body(eng):
                    waited = {}
                    for i in per_eng[e]:
                        ins = instrs[i]
                        need = {}
                        for j in deps[i]:
                            key, v = ticket[j]
                            need[key] = max(need.get(key, 0), v)
                        if ins["dma"] and prev_same_dma[i] is not None:
                            key, v = ticket[prev_same_dma[i]]
                            need[key] = max(need.get(key, 0), v)
                        for key, v in need.items():
                            if waited.get(key, 0) < v:
                                eng.wait_ge(sems[key], v)
                                waited[key] = v
                        bi = ins["fn"](eng)
                        if ticket[i] is not None:
                            key, v = ticket[i]
                            bi.then_inc(sems[key], 16 if ins["dma"] else 1)
                    for key, v in final_vals.items():
                        if key[0] == "dma" and key[1] == e and waited.get(key, 0) < v:
                            eng.wait_ge(sems[key], v)
                return body

            for e in ENGINES:
                if per_eng[e]:
                    getattr(block, e)(make(e))
        return self.stats


def _rel_bucket(dist):
    n = np.maximum(dist, 0)
    nf = np.maximum(n, 16).astype(np.float32)
    log_b = 16 + (np.log(nf / np.float32(16)) / np.float32(math.log(128 / 16)) * np.float32(16)).astype(np.int32)
    return np.where(n < 16, n, np.minimum(log_b, 31)).astype(np.int64)


def _onehot_dist(dist, valid):
    oh = np.zeros((32, dist.size), np.float32)
    b = _rel_bucket(dist)
    idx = np.nonzero(valid)[0]
    oh[b[idx], idx] = 1.0
    return oh


def make_consts(par):
    c = {}
    c["ident"] = np.eye(128, dtype=np.float32)
    s = np.arange(128)[:, None]
    t = np.arange(128)[None, :]
    tri = (s <= t).astype(np.float32)
    ones = np.ones((128, 128), np.float32)
    zeros = np.zeros((128, 128), np.float32)
    c["m0"] = tri if par == 0 else ones
    c["m1"] = zeros if par == 0 else tri
    ind = np.zeros((64, 4096), np.float32)
    ind[np.arange(4096) // 64, np.arange(4096)] = BIG
    c["ind"] = ind
    dw = np.arange(1023) - 255 + par * 128
    vw = (dw >= 0) & (dw < 512)
    c["ohw"] = _onehot_dist(dw, vw)
    c["wmask"] = np.broadcast_to(vw.astype(np.float32)[None, :], (128, 1023)).copy()
    ds = np.arange(511) - 255 + par * 128
    vs = ds >= 0
    c["ohs"] = _onehot_dist(ds, vs)
    c["smask"] = np.broadcast_to(vs.astype(np.float32)[None, :], (128, 511)).copy()
    q = np.arange(128)
    ohc = np.zeros((32, 25 * 128), np.float32)
    cm = np.zeros((128, 25), np.float32)
    for k in range(25):
        d = par * 128 + q - 16 * (k - 10) - 31
        v = d >= 0
        ohc[:, k * 128:(k + 1) * 128] = _onehot_dist(d, v)
        cm[:, k] = np.where(v, 0.0, -BIG)
    c["ohc"] = ohc
    c["cm"] = cm
    vmadd = np.zeros((16, 128, 128), np.float32)
    j = np.arange(64)[None, :]
    for g in range(16):
        qpos = (2 * g + par) * 128 + np.arange(128)[:, None]
        cur = qpos // 64
        forced = (j == 0) | (j == cur) | (j == cur - 1)
        valid = j * 64 <= qpos
        vm = (valid & ~forced).astype(np.float32)
        add = np.where(forced, 1e4, np.where(valid, 0.0, -1.0)).astype(np.float32)
        vmadd[g, :, 0:64] = vm
        vmadd[g, :, 64:128] = add
    c["vmadd"] = vmadd
    k_ = np.arange(128)[:, None]
    p_ = np.arange(128)[None, :]
    c["tri"] = (k_ <= p_).astype(np.float32)
    c["ones"] = ones.copy()
    return c


IN_SHAPES = dict(
    xf=[4096, 1024], xo=[2048, 1024], wkv=[1024, 1800], wq=[1024, 1048], wg=[1024, 2048],
    gat=[128, 8], gml=[128, 8], gfin=[1, 1024], bfg=[1, 8], tbl=[32, 8],
    pek=[32, 64], pev=[32, 64], w1k=[2048, 256], w1v=[2048, 256], w2k=[256, 64], w2v=[256, 64],
    wbf=[512, 1024], wbn=[512, 1024], wo=[1024, 1024], wf1=[1024, 4096], wf2=[4096, 1024],
    ident=[128, 128], m0=[128, 128], m1=[128, 128], ind=[64, 4096], ohw=[32, 1023], wmask=[128, 1023],
    ohs=[32, 511], smask=[128, 511], ohc=[32, 3200], cm=[128, 25], vmadd=[16, 128, 128],
    tri=[128, 128], ones=[128, 128],
)


class _Stop(Exception):
    pass


def build(nc, upto=None, nslots=16):
    P = Prog(nc)
    dumps = {}

    def dump(name, ap, shape, keys, dt=F32):
        if upto is None:
            return
        t = nc.dram_tensor("dbg_" + name, list(shape), dt, kind="ExternalOutput").ap()
        P.dma("sync", t, ap, keys, [])
        dumps[name] = shape

    def phase_end(name):
        if upto == name:
            raise _Stop()

    st = contextlib.ExitStack()
    try:
        _build_body(nc, P, dump, phase_end, nslots, st)
    except _Stop:
        pass
    stats = P.emit()
    st.close()
    stats["dumps"] = dumps
    return stats


def _build_body(nc, P, dump, phase_end, nslots, st):
    D = {k: nc.dram_tensor(k, list(v), F32, kind="ExternalInput").ap() for k, v in IN_SHAPES.items()}
    out = nc.dram_tensor("out", [2048, 1024], F32, kind="ExternalOutput").ap()
    ebs_w = nc.dram_tensor("ebs_w", [8 * 128, 1023], BF16, kind="Internal")
    ebs_s = nc.dram_tensor("ebs_s", [8 * 128, 511], BF16, kind="Internal")
    dbg_out = {}

    ARENA_BYTES = 212800
    arena = st.enter_context(nc.sbuf_tensor("arena", [128, ARENA_BYTES // 2], BF16))
    banks = [st.enter_context(nc.psum_tensor(f"bank{i}", [128, 512], F32)) for i in range(8)]

    class Alloc:
        def __init__(self, lo, hi):
            self.lo, self.hi, self.cur = lo, hi, lo

        def get(self, free_shape, dt):
            nel = int(np.prod(free_shape))
            nb = nel * (4 if dt == F32 else 2)
            nb = (nb + 63) // 64 * 64
            off = self.cur
            self.cur += nb
            assert self.cur <= self.hi, ("arena overflow", self.cur, self.hi)
            v = arena[:, off // 2:(off + nel * (4 if dt == F32 else 2)) // 2]
            if dt == F32:
                v = v.bitcast(F32)
            if len(free_shape) == 2:
                v = v.rearrange("p (a b) -> p a b", a=free_shape[0])
            elif len(free_shape) == 3:
                v = v.rearrange("p (a b c) -> p a b c", a=free_shape[0], b=free_shape[1])
            return v

    def bankv(i, dt=F32):
        b = banks[i][:, :]
        return b if dt == F32 else b.bitcast(BF16)

    A0 = Alloc(0, ARENA_BYTES)
    identb = A0.get([128], BF16)
    m0b = A0.get([128], BF16)
    m1b = A0.get([128], BF16)
    gat = A0.get([8], F32)
    gml = A0.get([8], F32)
    EPS_AP = A0.get([1], F32)
    ONE_AP = A0.get([1], F32)
    misc_end = A0.cur
    KF = A0.get([4, 4096], BF16)
    VF = A0.get([32, 8, 65], BF16)
    KS0 = A0.get([4096], BF16)
    KS1 = A0.get([4096], BF16)
    VS = A0.get([32, 2, 65], BF16)
    KW = A0.get([4096], BF16)
    VW = A0.get([32, 2, 65], BF16)
    NCt = A0.get([32, 8], F32)
    NCoff = A0.get([33, 8], F32)
    KCc = A0.get([256], BF16)
    VCA = A0.get([2, 2, 65], BF16)
    T31 = A0.get([8], F32)
    BA = A0.get([8, 25], F32)
    kv_end = A0.cur
    R0 = kv_end

    def load_const_bf(dst, src, key):
        P.dma("gpsimd", dst, src, [], [key])

    load_const_bf(identb, D["ident"][:, :], "identb")
    load_const_bf(m0b, D["m0"][:, :], "m0b")
    load_const_bf(m1b, D["m1"][:, :], "m1b")
    P.dma("sync", gat, D["gat"][:, :], [], ["gat"])
    P.dma("sync", gml, D["gml"][:, :], [], ["gml"])
    P.op("vector", I.memset(VF[:, :, :, 64:65], 1.0), [], ["VFone"])
    P.op("vector", I.memset(VS[:, :, :, 64:65], 1.0), [], ["VSone"])
    P.op("vector", I.memset(VW[:, :, :, 64:65], 1.0), [], ["VWone"])
    P.op("vector", I.memset(VCA[:, :, :, :], 0.0), [], ["VCA"])
    P.op("vector", I.memset(VCA[:, :, :, 64:65], 1.0), ["VCA"], ["VCA"])
    P.op("vector", I.memset(KCc[:, :], 0.0), [], ["KCc"])

    def norm_transpose(xsrc_ap, xt, sq, ss, rstd, xn, hT_dst, gcol, tag, psb, rkeys, wkey, dma=True, on_dve=False, on_pool=False):
        if dma:
            P.dma("sync", xt, xsrc_ap, [], [tag + "xt"])
        if on_dve:
            P.op("vector", I.scalar_tensor_tensor(xn, xt, 1.0, xt, ALU.mult, ALU.mult, accum_out=ss), [tag + "xt"], [tag + "xn", tag + "ss"])
        else:
            P.op("scalar", I.activation(xn, xt, AF.Square, accum_out=ss), [tag + "xt"], [tag + "xn", tag + "ss"])
        P.op("scalar", I.activation(rstd, ss, AF.Ln, bias=EPS_AP, scale=1.0 / 1024), [tag + "ss", "epsap"], [tag + "rms"])
        P.op("scalar", I.activation(rstd, rstd, AF.Exp, scale=-0.5), [tag + "rms"], [tag + "rstd"])
        if on_dve or on_pool:
            P.op("vector" if on_dve else "gpsimd", I.tensor_scalar(xn, xt, rstd, None, ALU.mult), [tag + "xt", tag + "rstd"], [tag + "xn"])
        else:
            P.op("scalar", I.activation(xn, xt, AF.Copy, scale=rstd), [tag + "xt", tag + "rstd"], [tag + "xn"])
        pb = bankv(psb, BF16)
        for d in range(8):
            P.tr(pb[:, d * 128:(d + 1) * 128], xn[:, d * 128:(d + 1) * 128], identb, [tag + "xn", "identb"], [f"B{psb}"])
        pv = pb.rearrange("p (d t) -> p d t", d=8)
        gb = gcol.rearrange("p (d o) -> p d o", o=1).broadcast_to([128, 8, 128])
        P.op("vector", I.tensor_tensor(hT_dst, pv, gb, ALU.mult), [f"B{psb}", "gat", "gml"] + list(rkeys), [wkey])

    P.op("vector", I.memset(EPS_AP, EPS), [], ["epsap"])
    P.op("vector", I.memset(ONE_AP, 1.0), [], ["oneap"])
    R0 = A0.cur

    AA = Alloc(R0, ARENA_BYTES)
    KCT = AA.get([4096], BF16)
    VCT = AA.get([4096], BF16)
    FLOG = AA.get([32, 8], F32)
    a_scratch = AA.cur
    WKV = AA.get([8, 1800], BF16)
    xts = [AA.get([1024], F32) for _ in range(4)]
    sqs = None
    xns = [AA.get([1024], BF16) for _ in range(4)]
    sss = [AA.get([1], F32) for _ in range(4)]
    rstds = [AA.get([1], F32) for _ in range(4)]
    hTs = [AA.get([8, 512], BF16) for _ in range(2)]

    wkv_v = D["wkv"].rearrange("(c p) n -> p c n", p=128)
    for d in range(8):
        P.dma("gpsimd", WKV[:, d, 1024:1800], wkv_v[:, d, 1024:1800], [], [f"WKV{d}b"])
    for d in range(8):
        P.dma("gpsimd", WKV[:, d, 0:1024], wkv_v[:, d, 0:1024], [], [f"WKV{d}a"])
    WKVK = [f"WKV{d}a" for d in range(8)]
    load_const_bf(KS0[64:128, :], D["ind"][:, :], "KS0i")
    load_const_bf(KS1[0:64, :], D["ind"][:, :], "KS1i")
    WKVKB = [f"WKV{d}b" for d in range(8)]
    CONV = {}
    conv_jobs = []
    for nm, (r_, c_) in (("wg", (1024, 2048)), ("wbf", (512, 1024)), ("wbn", (512, 1024)), ("wo", (1024, 1024)),
                         ("wf1", (1024, 4096)), ("wf2", (4096, 1024))):
        dst = nc.dram_tensor(nm + "_bf", [r_, c_], BF16, kind="Internal").ap()
        CONV[nm] = dst
        if c_ >= 2048:
            sv = D[nm].rearrange("r (a c) -> (r a) c", c=2048)
            dv = dst.rearrange("r (a c) -> (r a) c", c=2048)
        else:
            sv = D[nm].rearrange("(r a) c -> r (a c)", a=2048 // c_)
            dv = dst.rearrange("(r a) c -> r (a c)", a=2048 // c_)
        for i in range(sv.shape[0] // 128):
            conv_jobs.append((dv[i * 128:(i + 1) * 128, :], sv[i * 128:(i + 1) * 128, :], nm + "_bf"))

    xf_v = D["xf"].rearrange("(j p) n -> j p n", p=128)
    def nt_a(j):
        stl_, tt_ = j // 4, j % 4
        b_ = j % 4
        norm_transpose(xf_v[j], xts[b_], sqs, sss[b_], rstds[b_], xns[b_],
                       hTs[stl_ % 2][:, :, tt_ * 128:(tt_ + 1) * 128], gat, f"A{b_}", 7, [], f"hT{stl_ % 2}")

    nt_a(0)
    nt_a(1)
    for stl in range(8):
        hT = hTs[stl % 2]
        hkey = f"hT{stl % 2}"
        for tt in range(4):
            j = stl * 4 + tt
            b = j % 2
            for d in range(8):
                P.mm(bankv(0)[:, 0:512], hT[:, d, tt * 128:(tt + 1) * 128], WKV[:, d, 1024:1536], d == 0, d == 7,
                     [hkey] + WKVKB, ["B0"])
            for d in range(8):
                P.mm(bankv(1)[:, 0:264], hT[:, d, tt * 128:(tt + 1) * 128], WKV[:, d, 1536:1800], d == 0, d == 7,
                     [hkey] + WKVKB, ["B1"])
            if j + 2 < 32:
                nt_a(j + 2)
            P.op("scalar", I.activation(VF[:, j, :, 0:64], bankv(0)[:, 0:512].rearrange("p (h d) -> p h d", h=8), AF.Copy),
                 ["B0"], [("VF", j)])
            P.op("vector", I.tensor_copy(VS[:, j, :, 0:64], bankv(1)[:, 0:128].rearrange("p (h d) -> p h d", h=2)),
                 ["B1"], [("VS", j)])
            P.op("vector", I.tensor_copy(VW[:, j, :, 0:64], bankv(1)[:, 128:256].rearrange("p (h d) -> p h d", h=2)),
                 ["B1"], [("VW", j)])
            P.op("vector", I.tensor_copy(FLOG[:, j, :], bankv(1)[:, 256:264]), ["B1"], ["FLOG"])
        tsl = slice(stl * 512, (stl + 1) * 512)
        fm = [("fk", 0), ("fk", 1), ("fk", 2), ("fk", 3), ("kc", 4), ("vc", 5), ("ksl", 6), ("kwn", 7)]
        for n_, (kind, cc) in enumerate(fm):
            bk = 2 + (n_ % 2)
            for d in range(8):
                P.mm(bankv(bk)[:, 0:512], WKV[:, d, cc * 128:(cc + 1) * 128], hT[:, d, :], d == 0, d == 7,
                     [hkey] + WKVK, [f"B{bk}"])
            src = bankv(bk)[:, 0:512]
            if kind == "fk":
                P.op("scalar" if n_ % 2 else "vector",
                     (I.activation(KF[:, cc, tsl], src, AF.Copy)) if n_ % 2 else
                     (I.tensor_copy(KF[:, cc, tsl], src)),
                     [f"B{bk}"], [("KF", stl)])
            elif kind == "kc":
                P.op("vector", I.tensor_copy(KCT[:, tsl], src), [f"B{bk}"], ["KCT"])
            elif kind == "vc":
                P.op("scalar", I.activation(VCT[:, tsl], src, AF.Copy), [f"B{bk}"], ["VCT"])
            elif kind == "ksl":
                P.op("vector", I.tensor_copy(KS0[0:64, tsl], src[0:64, :]), [f"B{bk}"], [("KS0", stl)])
                P.op("vector", I.tensor_copy(KS1[64:128, tsl], src[64:128, :]), [f"B{bk}"], [("KS1", stl)])
            else:
                P.op("scalar", I.activation(KW[:, tsl], src, AF.Copy), [f"B{bk}"], [("KW", stl)])

    dump("KF", KF, [128, 4, 4096], [("KF", i) for i in range(8)], BF16)
    dump("VF", VF, [128, 32, 8, 65], [("VF", i) for i in range(32)] + ["VFone"], BF16)
    dump("KS0", KS0, [128, 4096], [("KS0", i) for i in range(8)] + ["KS0i"], BF16)
    dump("KW", KW, [128, 4096], [("KW", i) for i in range(8)], BF16)
    dump("FLOG", FLOG, [128, 32, 8], ["FLOG"])
    dump("KCT", KCT, [128, 4096], ["KCT"], BF16)
    phase_end("A")
    P.barrier()
    AD = Alloc(a_scratch, ARENA_BYTES)
    trif = AD.get([128], F32)
    onesf = AD.get([128], F32)
    bfrep = AD.get([8], F32)
    nlf = AD.get([32, 8], F32)
    P.dma("sync", trif, D["tri"][:, :], [], ["trif"])
    P.dma("sync", onesf, D["ones"][:, :], [], ["onesf"])
    P.dma("sync", bfrep, D["bfg"].partition_broadcast(128)[:, 0, :], [], ["bfrep"])
    P.op("vector", I.tensor_tensor(nlf, FLOG, bfrep.rearrange("p (o h) -> p o h", o=1).broadcast_to([128, 32, 8]), ALU.add),
         ["FLOG", "bfrep"], ["nlf"])
    P.op("scalar", I.activation(nlf, nlf, AF.Exp, scale=-1.0), ["nlf"], ["nlf"])
    P.op("scalar", I.activation(nlf, nlf, AF.Ln, bias=ONE_AP, scale=1.0), ["nlf", "oneap"], ["nlf"])
    nlf2 = nlf.rearrange("p j h -> p (j h)")
    P.mm(bankv(0)[:, 0:256], trif, nlf2, True, True, ["trif", "nlf"], ["B0"])
    P.mm(bankv(1)[:, 0:256], onesf, nlf2, True, True, ["onesf", "nlf"], ["B1"])
    tot = AD.get([32, 8], F32)
    P.op("vector", I.tensor_copy(tot, bankv(1)[:, 0:256].rearrange("p (j h) -> p j h", j=32)), ["B1"], ["tot"])
    P.op("vector", I.memset(NCoff[:, 0, :], 0.0), [], ["NCoff"])
    for j in range(32):
        P.op("vector", I.tensor_tensor(NCoff[:, j + 1, :], NCoff[:, j, :], tot[:, j, :], ALU.add),
             ["NCoff", "tot"], ["NCoff"])
    P.op("vector", I.tensor_tensor(NCt, bankv(0)[:, 0:256].rearrange("p (j h) -> p j h", j=32), NCoff[:, 0:32, :], ALU.add),
         ["B0", "NCoff"], ["NCt"])

    dump("NCt", NCt, [128, 32, 8], ["NCt"])
    dump("NCoff", NCoff, [128, 33, 8], ["NCoff"])
    phase_end("D")
    AC = Alloc(AD.cur, ARENA_BYTES)
    W1K = AC.get([32, 256], BF16)
    W1V = AC.get([32, 256], BF16)
    W2K2 = AC.get([2, 128], BF16)
    W2V = AC.get([2, 64], BF16)
    pef = AC.get([2, 32], F32)
    peb = AC.get([2, 32], BF16)
    peW = AC.get([2, 2], F32)
    a1T = AC.get([4, 2, 256], BF16)
    KCTb = AC.get([16, 256], BF16)
    VCTb = AC.get([16, 256], BF16)
    P.op("vector", I.tensor_copy(KCTb, KCT.rearrange("p (c b) -> p b c", b=16)), ["KCT"], ["KCTb"])
    P.op("gpsimd", I.tensor_copy(VCTb, VCT.rearrange("p (c b) -> p b c", b=16)), ["VCT"], ["VCTb"])
    for nm, W1 in (("w1k", W1K), ("w1v", W1V)):
        src = D[nm].rearrange("(l d) n -> d l n", d=64)
        for half in range(2):
            for lq in range(4):
                P.dma("gpsimd", W1[half * 64:(half + 1) * 64, lq * 8:(lq + 1) * 8, :], src[:, lq * 8:(lq + 1) * 8, :], [], [nm])
    w2k_v = D["w2k"].rearrange("(c p) n -> p c n", p=128)
    P.dma("gpsimd", W2K2[:, :, 0:64], w2k_v, [], ["W2K2"])
    P.dma("gpsimd", W2K2[:, :, 64:128], w2k_v, [], ["W2K2"])
    P.dma("gpsimd", W2V[:, :, :], D["w2v"].rearrange("(c p) n -> p c n", p=128), [], ["W2V"])
    P.dma("sync", pef[0:64, 0, :], D["pek"].rearrange("l d -> d l"), [], ["pef"], allow_slow_non_contiguous=True)
    P.dma("sync", pef[0:64, 1, :], D["pev"].rearrange("l d -> d l"), [], ["pef"], allow_slow_non_contiguous=True)
    P.op("vector", I.tensor_copy(peb[0:64], pef[0:64]), ["pef"], ["peb"])
    P.op("vector", I.memset(a1T[:, :, :, :], 0.0), [], ["a1T"])
    for kv, (W1, nm) in enumerate(((W1K, "w1k"), (W1V, "w1v"))):
        for hc in range(2):
            for l in range(32):
                P.mm(bankv(2)[:, kv * 2 + hc:kv * 2 + hc + 1], W1[0:64, l, hc * 128:(hc + 1) * 128], peb[0:64, kv, l:l + 1],
                     l == 0, l == 31, [nm, "peb"], ["B2"])
    P.op("vector", I.tensor_copy(peW, bankv(2)[:, 0:4].rearrange("p (a b) -> p a b", a=2)), ["B2"], ["peW"])
    nbk = 0
    for kv, (W1, nm, SRC, skey) in enumerate(((W1K, "w1k", KCTb, "KCTb"), (W1V, "w1v", VCTb, "VCTb"))):
        for hc in range(2):
            bks = (3 + (nbk % 4), 3 + ((nbk + 1) % 4))
            nbk += 2
            for l in range(32):
                for g in range(2):
                    r0 = g * 64
                    P.mm(bankv(bks[g])[:, 0:255], W1[r0:r0 + 64, l, hc * 128:(hc + 1) * 128],
                         SRC[r0:r0 + 64, l % 16, l // 16:l // 16 + 255], l == 0, l == 31, [nm, skey], [f"B{bks[g]}"])
            for g in range(2):
                P.op("scalar", I.activation(
                    a1T[:, kv * 2 + g, hc, 0:255], bankv(bks[g])[:, 0:255], AF.Silu, bias=peW[:, kv, hc:hc + 1]),
                    [f"B{bks[g]}", "peW", "a1T"], ["a1T"])
    for g in range(2):
        for hc in range(2):
            P.mm(bankv(0)[:, 0:255], W2K2[:, hc, :], a1T[:, g, hc, 0:255], hc == 0, hc == 1, ["W2K2", "a1T"], ["B0"])
        P.op("vector", I.tensor_copy(KCc[g * 64:(g + 1) * 64, 0:255], bankv(0)[g * 64:(g + 1) * 64, 0:255]),
             ["B0", "KCc"], ["KCc"])
        for ch in range(2):
            for hc in range(2):
                P.mm(bankv(1)[:, 0:64], a1T[:, 2 + g, hc, ch * 128:(ch + 1) * 128], W2V[:, hc, :], hc == 0, hc == 1,
                     ["W2V", "a1T"], ["B1"])
            P.op("vector", I.tensor_copy(VCA[:, ch, g, 0:64], bankv(1)[:, 0:64]), ["B1", "VCA"], ["VCA"])

    dump("KCc", KCc, [128, 256], ["KCc"], BF16)
    dump("VCA", VCA, [128, 2, 2, 65], ["VCA"], BF16)
    phase_end("C")
    AE = Alloc(R0, ARENA_BYTES)
    P.barrier()
    WQ = AE.get([8, 1048], BF16)
    ot_lo = AE.cur
    OTF = AE.get([4, 2048], BF16)
    OTN = AE.get([4, 2048], BF16)
    ebw_lo = AE.cur
    EBW = AE.get([6, 8, 128], BF16)
    EBS = AE.get([3, 8, 128], BF16)
    AT_ = Alloc(AE.cur, ARENA_BYTES)
    tblf = AT_.get([8], F32)
    tbld = AT_.get([8], F32)
    t31r = AT_.get([8], F32)
    ones32 = AT_.get([128], F32)
    tblreps = [AT_.get([128], F32) for _ in range(2)]
    AT2 = Alloc(ot_lo, ot_lo + 32768)
    ohw = AT2.get([1023], F32)
    ohs = AT2.get([511], F32)
    ohc = AT2.get([3200], F32)
    wmask = AT_.get([1023], F32)
    smask = AT_.get([511], F32)
    cmk = AT_.get([25], F32)
    ebrows = [AT_.get([1024], F32) for _ in range(2)]
    ebrowbs = [AT_.get([1024], BF16) for _ in range(2)]
    P.dma("sync", tblf[0:32, :], D["tbl"][:, :], [], ["tblf"])
    P.dma("sync", t31r[0:32, :], D["tbl"][31:32, :].partition_broadcast(32)[:, 0, :], [], ["t31r"])
    P.dma("sync", T31, D["tbl"][31:32, :].partition_broadcast(128)[:, 0, :], [], ["T31"])
    P.dma("sync", ohw[0:32, :], D["ohw"][:, :], [], ["ohw"])
    P.dma("sync", ohs[0:32, :], D["ohs"][:, :], [], ["ohs"])
    P.dma("sync", ohc[0:32, :], D["ohc"][:, :], [], ["ohc"])
    P.dma("sync", wmask, D["wmask"][:, :], [], ["wmask"])
    P.dma("sync", smask, D["smask"][:, :], [], ["smask"])
    P.dma("sync", cmk, D["cm"][:, :], [], ["cmk"])
    P.op("vector", I.memset(ones32[0:32, :], 1.0), [], ["ones32"])
    P.op("vector", I.tensor_tensor(tbld[0:32, :], tblf[0:32, :], t31r[0:32, :], ALU.subtract), ["tblf", "t31r"], ["tbld"])
    for k in range(25):
        P.mm(bankv(0)[:, k * 8:(k + 1) * 8], ohc[0:32, k * 128:(k + 1) * 128], tblf[0:32, :], True, True, ["ohc", "tblf"], ["B0"])
    P.op("vector", I.tensor_tensor(BA.rearrange("p h k -> p k h"), bankv(0)[:, 0:200].rearrange("p (k h) -> p k h", k=25),
                                             cmk.rearrange("p (k o) -> p k o", o=1).broadcast_to([128, 25, 8]), ALU.add),
         ["B0", "cmk"], ["BA"])
    ebsw_ap = ebs_w.ap()
    ebss_ap = ebs_s.ap()
    for h in range(8):
        tblrep, ebrow, ebrowb = tblreps[0], ebrows[0], ebrowbs[0]
        P.op("vector", I.tensor_scalar(tblrep[0:32, :], ones32[0:32, :], tblf[0:32, h:h + 1], None, ALU.mult),
             ["ones32", "tblf"], ["tblrep0"])
        P.mm(bankv(1)[:, 0:512], tblrep[0:32, :], ohw[0:32, 0:512], True, True, ["tblrep0", "ohw"], ["B1"])
        P.mm(bankv(2)[:, 0:511], tblrep[0:32, :], ohw[0:32, 512:1023], True, True, ["tblrep0", "ohw"], ["B2"])
        P.op("scalar", I.activation(ebrow[:, 0:512], bankv(1)[:, 0:512], AF.Exp), ["B1"], ["ebrow0"])
        P.op("scalar", I.activation(ebrow[:, 512:1023], bankv(2)[:, 0:511], AF.Exp), ["B2", "ebrow0"], ["ebrow0"])
        P.op("vector", I.tensor_tensor(ebrowb[:, 0:1023], ebrow[:, 0:1023], wmask, ALU.mult), ["ebrow0", "wmask"], ["ebrowb0"])
        P.dma("sync", ebsw_ap[h * 128:(h + 1) * 128, :], ebrowb[:, 0:1023], ["ebrowb0"], [("ebs_w", h)])
        for k in range(6):
            src = bass.AP(ebs_w, h * 128 * 1023 + (4 - k) * 128 + 255, [[1022, 128], [1, 128]])
            P.dma("sync", EBW[:, k, h, :], src, [("ebs_w", h)], ["EBW"])
        tblrep, ebrow, ebrowb = tblreps[1], ebrows[1], ebrowbs[1]
        P.op("vector", I.tensor_scalar(tblrep[0:32, :], ones32[0:32, :], tbld[0:32, h:h + 1], None, ALU.mult),
             ["ones32", "tbld"], ["tblrep1"])
        P.mm(bankv(3)[:, 0:511], tblrep[0:32, :], ohs[0:32, 0:511], True, True, ["tblrep1", "ohs"], ["B3"])
        P.op("scalar", I.activation(ebrow[:, 0:511], bankv(3)[:, 0:511], AF.Exp), ["B3"], ["ebrow1"])
        P.op("vector", I.tensor_tensor(ebrowb[:, 0:511], ebrow[:, 0:511], smask, ALU.mult), ["ebrow1", "smask"], ["ebrowb1"])
        P.dma("sync", ebss_ap[h * 128:(h + 1) * 128, :], ebrowb[:, 0:511], ["ebrowb1"], [("ebs_s", h)])
        for k in range(3):
            src = bass.AP(ebs_s, h * 128 * 511 + (1 - k) * 128 + 255, [[510, 128], [1, 128]])
            P.dma("sync", EBS[:, k, h, :], src, [("ebs_s", h)], ["EBS"])

    wq_v = D["wq"].rearrange("(c p) n -> p c n", p=128)
    for d in range(8):
        P.dma("gpsimd", WQ[:, d, :], wq_v[:, d, :], [], ["WQ"])
    dump("EBW", EBW, [128, 6, 8, 128], ["EBW"], BF16)
    dump("EBS", EBS, [128, 3, 8, 128], ["EBS"], BF16)
    dump("BA", BA, [128, 8, 25], ["BA"])
    phase_end("T")
    P.barrier()

    AS = Alloc(AE.cur, ARENA_BYTES)
    xto = [AS.get([1024], F32)] * 2
    sqo = None
    xno = AS.get([1024], BF16)
    sso = AS.get([1], F32)
    rso = AS.get([1], F32)
    hTo = AS.get([8, 128], BF16)
    QF = AS.get([4, 256], BF16)
    QA = AS.get([8, 128], BF16)
    sigs = [AS.get([24], F32) for _ in range(2)]
    wgt = AS.get([32, 8], F32)
    wgb = AS.get([32, 8], BF16)
    imp2 = wgt[:, 0:8, :].rearrange("p a b -> p (a b)")
    Pts = [AS.get([512], BF16) for _ in range(5)]
    rl = AS.get([1, 8], F32)
    coef = AS.get([1, 8], F32)
    ofb = AS.get([8, 64], BF16)
    onf = AS.get([8, 64], F32)
    onb = AS.get([8, 64], BF16)
    sb = hTo.rearrange("p d t -> p (d t)").bitcast(F32).rearrange("p (a b) -> p a b", a=2)
    ebf = AS.get([4, 256], BF16)
    lsum = AS.get([8], F32)
    rlc = AS.get([8], F32)
    accp = AS.get([264], F32)
    imp = AS.get([64], F32)
    m8 = AS.get([16], F32)
    PN = AS.get([128], BF16)
    vma = [AS.get([128], F32)] * 2
    eT = xno.rearrange("p (r c t) -> p r c t", r=4, c=2)
    VPs = [AS.get([8, 65], BF16) for _ in range(4)]
    P.op("vector", I.memset(QF[:, :, :], 0.0), [], ["QF"])
    P.op("vector", I.memset(ebf[:, :, :], 0.0), [], ["ebf"])
    P.op("vector", I.memset(accp, 0.0), [], ["accp"])

    xo_v = D["xo"].rearrange("(j p) n -> j p n", p=128)
    sbank_ctr = [0]

    def sbank(pool=(0, 1, 2)):
        b = pool[sbank_ctr[0] % len(pool)]
        sbank_ctr[0] += 1
        return b

    pt_ctr = [0]

    def next_pt():
        i = pt_ctr[0] % 5
        pt_ctr[0] += 1
        return i

    def kfkey(j):
        return ("KF", j // 4)

    def prologue_a(g):
        norm_transpose(xo_v[g], xto[0], sqo, sso, rso, xno, hTo[:, :, :], gat, "E0", 7, [], "hTo", dma=(g == 0), on_dve=True)
        yield
        bq = 7
        for c in range(4):
            for d in range(8):
                P.mm(bankv(bq)[:, c * 128:(c + 1) * 128], WQ[:, d, c * 128:(c + 1) * 128], hTo[:, d, :], d == 0, d == 7,
                     ["WQ", "hTo"], [f"B{bq}"])
            yield
        P.op("vector", I.tensor_copy(QF[:, :, 0:128], bankv(bq)[:, 0:512].rearrange("p (c t) -> p c t", c=4)), [f"B{bq}", "QF"], ["QF"])

    def decay_weights(g):
        nj_ = 2 * g + 2
        P.op("vector", I.tensor_tensor(wgt[:, 0:nj_, :], NCt[:, 0:nj_, :], NCoff[:, 2 * g:2 * g + 1, :].broadcast_to([128, nj_, 8]), ALU.subtract),
             ["NCt", "NCoff", "wgt"], ["wgt"])
        P.op("scalar", I.activation(wgb[:, 0:nj_, :], wgt[:, 0:nj_, :], AF.Exp), ["wgt", "wgb"], ["wgb"])

    def prologue_b(g):
        sig = sigs[g % 2]
        skey = f"sig{g % 2}"
        bq2 = sbank()
        for c in range(4):
            for d in range(8):
                P.mm(bankv(bq2)[:, c * 128:(c + 1) * 128], WQ[:, d, 512 + c * 128:512 + (c + 1) * 128], hTo[:, d, :], d == 0, d == 7,
                     ["WQ", "hTo"], [f"B{bq2}"])
        P.op("vector", I.tensor_copy(QA[0:64, 0:4, :], bankv(bq2)[0:64, 0:512].rearrange("p (c t) -> p c t", c=4)),
             [f"B{bq2}"], ["QAq"])
        P.op("vector", I.tensor_copy(QA[64:128, 4:8, :], bankv(bq2)[64:128, 0:512].rearrange("p (c t) -> p c t", c=4)),
             [f"B{bq2}"], ["QAq"])
        bq3 = sbank()
        for d in range(8):
            P.mm(bankv(bq3)[:, 0:24], hTo[:, d, :], WQ[:, d, 1024:1048], d == 0, d == 7, ["WQ", "hTo"], [f"B{bq3}"])
        P.op("scalar", I.activation(sig, bankv(bq3)[:, 0:24], AF.Exp, scale=-1.0), [f"B{bq3}"], [skey])
        P.op("vector", I.tensor_scalar(sig, sig, 1.0, None, ALU.add), [skey], [skey])
        P.op("vector", I.reciprocal(sig, sig), [skey], [skey])

    for _ in prologue_a(0):
        pass
    prologue_b(0)
    decay_weights(0)

    pending_tail = [None]
    for g in range(nslots):
        for _ in range(3 if nslots == 16 else len(conv_jobs)):
            if conv_jobs:
                dv_, sv_, key_ = conv_jobs.pop(0)
                P.dma("gpsimd", dv_, sv_, [], [key_])
        sig = sigs[g % 2]
        skey = f"sig{g % 2}"
        sig3 = sig.rearrange("p (h b) -> p h b", b=3)

        def fin(obanks, br):
            for n_, ob in enumerate(obanks):
                ov = bankv(ob)[:, 0:260].rearrange("p (h d) -> p h d", h=4)
                P.op("vector", I.tensor_scalar(rl[:, 0, n_ * 4:(n_ + 1) * 4], ov[:, :, 64], 1e-30, None, ALU.max), [f"B{ob}", "rl"], ["rl"])
                P.op("vector", I.reciprocal(rl[:, 0, n_ * 4:(n_ + 1) * 4], rl[:, 0, n_ * 4:(n_ + 1) * 4]), ["rl"], ["rl"])
            if br is None:
                for n_, ob in enumerate(obanks):
                    ov = bankv(ob)[:, 0:260].rearrange("p (h d) -> p h d", h=4)
                    cv = rl[:, 0, n_ * 4:(n_ + 1) * 4].rearrange("p (h o) -> p h o", o=1).broadcast_to([128, 4, 64])
                    P.op("vector", I.tensor_tensor(ofb[:, n_ * 4:(n_ + 1) * 4, :], ov[:, :, 0:64], cv, ALU.mult),
                         [f"B{ob}", "rl", "ofb"], ["ofb"])
                return
            P.op("vector", I.tensor_tensor(coef[:, 0, :], rl[:, 0, :], sig3[:, :, br], ALU.mult), ["rl", skey], ["coef"])
            for n_, ob in enumerate(obanks):
                ov = bankv(ob)[:, 0:260].rearrange("p (h d) -> p h d", h=4)
                if br == 0:
                    cv = coef[:, 0, n_ * 4:(n_ + 1) * 4].rearrange("p (h o) -> p h o", o=1).broadcast_to([128, 4, 64])
                    P.op("vector", I.tensor_tensor(onf[:, n_ * 4:(n_ + 1) * 4, :], ov[:, :, 0:64], cv, ALU.mult),
                         [f"B{ob}", "coef", "onf"], ["onf"])
                else:
                    for hh in range(4):
                        h = n_ * 4 + hh
                        P.op("vector", I.scalar_tensor_tensor(onf[:, h, :], ov[:, hh, 0:64], coef[:, 0, h:h + 1], onf[:, h, :], ALU.mult, ALU.add),
                             [f"B{ob}", "coef", "onf"], ["onf"])

        phase_end("Eq")
        ncol = min(255, 16 * g + 15)
        c_lo = max(0, 16 * g - 10)
        k_lo = c_lo - (16 * g - 10)
        nk = ncol - c_lo
        nch = 1 if ncol <= 128 else 2
        nj = 2 * g + 2

        def cmp_gen(g=g, ncol=ncol, c_lo=c_lo, k_lo=k_lo, nk=nk, nch=nch, tail=None):
            hp = 4 if ncol <= 128 else 2
            cs = 512 // hp
            sbA_ = sb.rearrange("p a b -> p (a b)").rearrange("p (a b) -> p a b", a=hp)
            sbB_ = xto[0][:, 0:512].rearrange("p (a b) -> p a b", a=hp)
            bufs = ((sbA_, "hTo"), (sbB_, "E0xt"))
            P.dma("sync", vma[0], D["vmadd"][g], [], ["vma"])

            def heads_of(p):
                if hp == 4:
                    return p * 4, [p * 4 + r for r in range(4)]
                return (p // 2) * 4, [(p // 2) * 4 + (p % 2) * 2 + r for r in range(2)]

            def A(p):
                gbase, hs = heads_of(p)
                r0 = (gbase // 4) * 64
                for rr, h in enumerate(hs):
                    P.mm(bankv(7)[:, rr * cs:rr * cs + ncol], QA[r0:r0 + 64, h, :], KCc[r0:r0 + 64, 0:ncol], True, True,
                         ["QAq", "KCc"], ["B7"])

            def B(p):
                gbase, hs = heads_of(p)
                sbp, sk = bufs[p % 2]
                for rr, h in enumerate(hs):
                    col0 = rr * cs
                    if c_lo > 0:
                        P.op("vector", I.tensor_scalar(sbp[:, rr, 0:c_lo], bankv(7)[:, col0:col0 + c_lo], SCALE, T31[:, h:h + 1], ALU.mult, ALU.add),
                             ["B7", "T31", sk], [sk])
                    P.op("vector", I.scalar_tensor_tensor(sbp[:, rr, c_lo:ncol], bankv(7)[:, col0 + c_lo:col0 + ncol], SCALE,
                                                          BA[:, h, k_lo:k_lo + nk], ALU.mult, ALU.add), ["B7", "BA", sk], [sk])

            def C(p):
                gbase, hs = heads_of(p)
                sbp, sk = bufs[p % 2]
                for rr, h in enumerate(hs):
                    P.op("scalar", I.activation(sbp[:, rr, 0:ncol], sbp[:, rr, 0:ncol], AF.Exp, accum_out=lsum[:, h:h + 1]), [sk], [sk, "lsum"])

            def Dd(p):
                gbase, hs = heads_of(p)
                sbp, sk = bufs[p % 2]
                h0 = hs[0]
                r_lo = h0 - gbase
                P.op("gpsimd", I.tensor_copy(ebf[:, r_lo:r_lo + len(hs), 0:ncol], sbp[:, :, 0:ncol]), [sk, "ebf"], ["ebf"])
                P.op("vector", I.tensor_scalar(rlc[:, h0:h0 + len(hs)], lsum[:, h0:h0 + len(hs)], 1e-30, None, ALU.max), ["lsum", "rlc"], ["rlc"])
                P.op("vector", I.reciprocal(rlc[:, h0:h0 + len(hs)], rlc[:, h0:h0 + len(hs)]), ["rlc"], ["rlc"])
                for rr, h in enumerate(hs):
                    if h == gbase:
                        P.op("vector", I.tensor_scalar(accp[:, 1:1 + ncol], sbp[:, rr, 0:ncol], rlc[:, h:h + 1], None, ALU.mult), [sk, "rlc", "accp"], ["accp"])
                    else:
                        P.op("vector", I.scalar_tensor_tensor(accp[:, 1:1 + ncol], sbp[:, rr, 0:ncol], rlc[:, h:h + 1], accp[:, 1:1 + ncol], ALU.mult, ALU.add),
                             [sk, "rlc", "accp"], ["accp"])

            def E(grp):
                P.op("vector", I.tensor_reduce(imp, accp[:, 1:257].rearrange("p (j f) -> p j f", f=4), AX.X, ALU.add), ["accp"], ["imp"])
                P.op("vector", I.tensor_tensor(imp, imp, accp[:, 0:256:4], ALU.add), ["imp", "accp"], ["imp"])
                P.op("vector", I.tensor_tensor(imp, imp, vma[0][:, 0:64], ALU.mult), ["imp", "vma"], ["imp"])
                P.op("vector", I.tensor_tensor(imp, imp, vma[0][:, 64:128], ALU.add), ["imp", "vma"], ["imp"])

            def F(grp):
                P.op("vector", I.max(m8[:, 0:8], imp), ["imp"], ["m8"])
                P.op("vector", I.match_replace(imp2, m8[:, 0:8], imp, -1e30), ["imp", "m8", "wgt"], ["wgt"])
                P.op("vector", I.max(m8[:, 8:16], imp2), ["wgt", "m8"], ["m8"])
                pc0 = 64 if grp == 0 else 0
                P.op("vector", I.tensor_scalar(PN[:, pc0:pc0 + 64], imp, m8[:, 15:16], 1.0, ALU.is_ge, ALU.subtract), ["imp", "m8", "PN"], ["PN"])

            def G(grp):
                for r in range(4):
                    for ch in range(nch):
                        P.tr(bankv(7, BF16)[:, (r * 2 + ch) * 128:(r * 2 + ch + 1) * 128], ebf[:, r, ch * 128:(ch + 1) * 128], identb,
                             ["ebf", "identb"], ["B7"])

            def H(grp):
                P.op("vector", I.tensor_copy(eT[:, :, 0:nch, :],
                                             bankv(7, BF16)[:, 0:1024].rearrange("p (r c t) -> p r c t", r=4, c=2)[:, :, 0:nch, :]), ["B7", "E0xn"], ["E0xn"])

            def Ii(grp):
                for r in range(4):
                    h = grp * 4 + r
                    for ch in range(nch):
                        P.mm(bankv(6)[:, h * 64:h * 64 + 64], eT[:, r, ch, :], VCA[:, ch, grp, 0:64], (h == 0 and ch == 0), False,
                             ["E0xn", "VCA"], ["B6"])

            def J(_):
                P.op("vector", I.tensor_tensor(coef[:, 0, :], rlc, sig3[:, :, 0], ALU.mult), ["rlc", skey], ["coef"])
                P.op("vector", I.tensor_tensor(onf, bankv(6)[:, 0:512].rearrange("p (h d) -> p h d", h=8),
                                               coef[:, 0, :].rearrange("p (h o) -> p h o", o=1).broadcast_to([128, 8, 64]), ALU.mult),
                     ["B6", "coef", "onf"], ["onf"])

            def K(_):
                P.tr(bankv(7, BF16)[:, 0:128], PN, identb, ["PN", "identb"], ["B7"])

            def L(_):
                P.op("vector", I.tensor_copy(QA[64:128, 0:4, :], bankv(7, BF16)[64:128, 0:128].rearrange("p (o t) -> p o t", o=1).broadcast_to([64, 4, 128])),
                     ["B7"], ["QAp"])
                P.op("vector", I.tensor_copy(QA[0:64, 4:8, :], bankv(7, BF16)[0:64, 0:128].rearrange("p (o t) -> p o t", o=1).broadcast_to([64, 4, 128])),
                     ["B7"], ["QAp"])

            if hp == 2:
                order = [(A, 0), (B, 0), (A, 1), (C, 0), (B, 1), (A, 2), (Dd, 0), (C, 1), (B, 2), (A, 3), (Dd, 1), (C, 2), (B, 3),
                         (G, 0), (E, 0), (H, 0), (Dd, 2), (F, 0), (C, 3), (Ii, 0), (Dd, 3), (E, 1), (F, 1), (G, 1), (H, 1), (Ii, 1),
                         (J, 0), (K, 0), (L, 0)]
            else:
                order = [(A, 0), (B, 0), (A, 1), (C, 0), (B, 1), (Dd, 0), (G, 0), (E, 0), (C, 1), (H, 0), (F, 0), (Dd, 1), (Ii, 0),
                         (E, 1), (F, 1), (G, 1), (H, 1), (Ii, 1), (J, 0), (K, 0), (L, 0)]
            splice_at = max(i_ for i_, (f_, _a) in enumerate(order) if f_ is B)
            for n_, (fn_, arg_) in enumerate(order):
                fn_(arg_)
                yield
                if fn_ in (A, K) or n_ >= len(order) - 8:
                    yield
                if n_ == splice_at and tail is not None:
                    yield from tail


        pend = []
        DEPTH = 4

        def push(fn):
            pend.append(fn)
            while len(pend) > DEPTH:
                pend.pop(0)()

        def flush():
            while pend:
                pend.pop(0)()

        def attend(tiles, batches, obanks, post, filler=None, fin_args=None, stride=1, drain=True, spool=(0, 1, 2), after_qk=None, interleave_rows=False, burst=1, spool2=None, spool2_from=0):
            first = {b: True for b in obanks}
            gi = 0
            for ti, tile in enumerate(tiles):
                if "pre" in tile:
                    tile["pre"]()
                pre_banks = None
                if interleave_rows:
                    pre_banks = [sbank(spool) for _ in batches]
                    if "qkg" in tile:
                        for bi, heads_ in enumerate(batches):
                            for (c0, ncols, lhsT, rhs, rk) in tile["qkg"](heads_):
                                P.mm(bankv(pre_banks[bi])[:, c0:c0 + ncols], lhsT, rhs, True, True, rk, [f"B{pre_banks[bi]}"])
                    else:
                        for hh in range(4):
                            for bi, heads_ in enumerate(batches):
                                lhsT, rhs, rk = tile["qk"](heads_[hh])
                                P.mm(bankv(pre_banks[bi])[:, hh * 128:(hh + 1) * 128], lhsT, rhs, True, True, rk, [f"B{pre_banks[bi]}"])
                for bi, heads in enumerate(batches):
                    if pre_banks is not None:
                        bk = pre_banks[bi]
                    else:
                        bk = sbank(spool2 if (spool2 is not None and gi >= spool2_from) else spool)
                        if "qkg" in tile:
                            for (c0, ncols, lhsT, rhs, rk) in tile["qkg"](heads):
                                P.mm(bankv(bk)[:, c0:c0 + ncols], lhsT, rhs, True, True, rk, [f"B{bk}"])
                        else:
                            for hh, h in enumerate(heads):
                                lhsT, rhs, rk = tile["qk"](h)
                                P.mm(bankv(bk)[:, hh * 128:(hh + 1) * 128], lhsT, rhs, True, True, rk, [f"B{bk}"])
                    pi = next_pt()
                    Pt = Pts[pi]
                    pk = f"Pt{pi}"
                    P.op("scalar", I.activation(Pt, bankv(bk)[:, 0:512], AF.Exp, scale=SCALE), [f"B{bk}"], [pk])
                    post(ti, tile, heads, Pt, pk)

                    def pv(tile=tile, heads=heads, Pt=Pt, pk=pk):
                        for hh, h in enumerate(heads):
                            ob = obanks[h // 4]
                            rhsv, vk = tile["v"](h)
                            P.mm(bankv(ob)[:, (h % 4) * 65:(h % 4) * 65 + 65], Pt[:, hh * 128:(hh + 1) * 128], rhsv, first[ob], False,
                                 [pk] + vk, [f"B{ob}"])
                            first[ob] = False
                    push(pv)
                    gi += 1
                    if filler is not None and gi % stride == 0:
                        for _ in range(burst):
                            next(filler, None)
            if filler is not None and drain:
                for _ in filler:
                    pass
            if after_qk is not None:
                after_qk()
            push(lambda: fin(*fin_args))

        fox_tiles = []
        for j in range(nj):
            def qk(h, j=j):
                r0 = (h % 2) * 64
                return (KF[r0:r0 + 64, h // 2, j * 128:(j + 1) * 128], QF[r0:r0 + 64, h // 2, 0:128], [kfkey(j), "QF"])

            def vv(h, j=j):
                return (VPs[j % 4][:, h, :], [f"VP{j % 4}"])

            def pre(j=j):
                wv = wgb[:, j, :].rearrange("p (h o) -> p h o", o=1).broadcast_to([128, 8, 65])
                P.op("vector" if j % 2 == 0 else "gpsimd", I.tensor_tensor(VPs[j % 4], VF[:, j, :, :], wv, ALU.mult),
                     [("VF", j), "VFone", "wgb", f"VP{j % 4}"], [f"VP{j % 4}"])
            fox_tiles.append(dict(j=j, qk=qk, v=vv, pre=pre))

        def fox_post(ti, tile, heads, Pt, pk):
            j = tile["j"]
            if j >= nj - 2:
                P3 = Pt.rearrange("p (h t) -> p h t", h=4)
                mk = m0b if j == nj - 2 else m1b
                mv = mk.rearrange("p (o t) -> p o t", o=1).broadcast_to([128, 4, 128])
                P.op("vector", I.tensor_tensor(P3, P3, mv, ALU.mult), [pk, "m0b", "m1b"], [pk])

        def chain_gen(*gens):
            for gen_ in gens:
                if gen_ is not None:
                    yield from gen_

        cmpf = cmp_gen(tail=pending_tail[0])
        pending_tail[0] = None
        n_stage = 44
        n_groups = 2 * nj + 2 * min(6, 2 * g + 2)
        cstride = max(1, n_groups // n_stage)
        cburst = max(1, -(-n_stage // n_groups))
        attend(fox_tiles, [[0, 2, 4, 6], [1, 3, 5, 7]], [3, 4], fox_post, filler=cmpf, fin_args=([3, 4], None), stride=cstride, drain=False, spool=(0, 1, 2, 5),
               interleave_rows=True, burst=cburst)
        phase_end("Efox")

        win_tiles = []
        for k in range(6):
            j = 2 * g - 4 + k
            if j < 0:
                continue

            def qkg(heads, j=j):
                gi_ = heads[0] // 4
                r0 = gi_ * 64
                return [(0, 512, KW[r0:r0 + 64, j * 128:(j + 1) * 128],
                         QA[r0:r0 + 64, 4 * gi_:4 * gi_ + 4, :].rearrange("p h t -> p (h t)"), [("KW", j // 4), "QAq"])]

            def vv(h, j=j):
                return (VW[:, j, h // 4, :], [("VW", j), "VWone"])
            win_tiles.append(dict(j=j, k=k, qkg=qkg, v=vv))

        def win_post(ti, tile, heads, Pt, pk):
            h0 = heads[0]
            k = tile["k"]
            P3 = Pt.rearrange("p (h t) -> p h t", h=4)
            P.op("vector", I.tensor_tensor(P3, P3, EBW[:, k, h0:h0 + 4, :], ALU.mult), [pk, "EBW"], [pk])

        attend(win_tiles, [[0, 1, 2, 3], [4, 5, 6, 7]], [3, 4], win_post, filler=cmpf, fin_args=([3, 4], 2), stride=cstride, drain=True, spool=(0, 1, 2, 5), interleave_rows=True, burst=cburst)
        if g + 1 < nslots:
            P.dma("sync", xto[0], xo_v[g + 1], [], ["E0xt"])
        slc_tiles = []
        for j in range(nj):
            def qkg(heads, j=j):
                gi_ = heads[0] // 4
                KSg = KS0 if gi_ == 0 else KS1
                return [(0, 512, KSg[:, j * 128:(j + 1) * 128], QA[:, 4 * gi_:4 * gi_ + 4, :].rearrange("p h t -> p (h t)"),
                         [("KS0", j // 4), ("KS1", j // 4), "KS0i", "KS1i", "QAq", "QAp"])]

            def vv(h, j=j):
                return (VS[:, j, h // 4, :], [("VS", j), "VSone"])
            slc_tiles.append(dict(j=j, qkg=qkg, v=vv))

        def slc_post(ti, tile, heads, Pt, pk):
            j = tile["j"]
            kk = j - (2 * g - 1)
            if kk >= 0:
                h0 = heads[0]
                P3 = Pt.rearrange("p (h t) -> p h t", h=4)
                P.op("vector", I.tensor_tensor(P3, P3, EBS[:, kk, h0:h0 + 4, :], ALU.mult), [pk, "EBS"], [pk])

        pf = prologue_a(g + 1) if g + 1 < nslots else None
        attend(slc_tiles, [[0, 1, 2, 3], [4, 5, 6, 7]], [5, 6], slc_post, filler=pf, fin_args=([5, 6], 1), stride=max(1, (2 * nj) // 6), drain=True, burst=max(1, -(-10 // (2 * nj))), spool=(0, 1, 2), spool2=(0, 1, 2, 3, 4), spool2_from=10,

               after_qk=(lambda: (decay_weights(g + 1), prologue_b(g + 1))) if g + 1 < nslots else None)
        phase_end("Eslc")

        flush()
        phase_end("Ewin")

        def tail_gen(g=g):
            P.op("vector", I.tensor_copy(onb, onf), ["onf"], ["onb"])
            yield
            yield
            ofb2 = ofb.rearrange("p h d -> p (h d)")
            onb2 = onb.rearrange("p h d -> p (h d)")
            for c in range(4):
                P.tr(bankv(7, BF16)[:, c * 128:(c + 1) * 128], ofb2[:, c * 128:(c + 1) * 128], identb, ["ofb", "identb"], ["B7"])
            for c in range(4):
                P.tr(bankv(7, BF16)[:, 512 + c * 128:512 + (c + 1) * 128], onb2[:, c * 128:(c + 1) * 128], identb, ["onb", "identb"], ["B7"])
            yield
            yield
            P.op("vector", I.tensor_copy(OTF[:, :, g * 128:(g + 1) * 128], bankv(7, BF16)[:, 0:512].rearrange("p (c t) -> p c t", c=4)),
                 ["B7"], ["OTF"])
            P.op("vector", I.tensor_copy(OTN[:, :, g * 128:(g + 1) * 128], bankv(7, BF16)[:, 512:1024].rearrange("p (c t) -> p c t", c=4)),
                 ["B7"], ["OTN"])
            yield

        pending_tail[0] = tail_gen()

    for _ in pending_tail[0]:
        pass

    dump("rlc", rlc, [128, 8], ["rlc"])
    dump("lsum", lsum, [128, 4], ["lsum"])
    dump("coef", coef, [128, 1, 8], ["coef"])
    dump("onf", onf, [128, 8, 64], ["onf"])
    phase_end("Efin")
    dump("OTF", OTF, [128, 4, 2048], ["OTF"], BF16)
    dump("OTN", OTN, [128, 4, 2048], ["OTN"], BF16)
    phase_end("E")
    P.barrier()
    AF1 = Alloc(misc_end, R0)
    WG = AF1.get([8, 2048], BF16)
    WBF = AF1.get([4, 1024], BF16)
    WBN = AF1.get([4, 1024], BF16)
    MT = Alloc(ebw_lo, ARENA_BYTES).get([8, 2048], BF16)
    hT4 = [AF1.get([8, 512], BF16) for _ in range(2)]
    xt1 = [AF1.get([1024], F32) for _ in range(2)]
    sq1 = None
    xn1 = [AF1.get([1024], BF16) for _ in range(2)]
    ss1 = [AF1.get([1], F32) for _ in range(2)]
    rs1 = [AF1.get([1], F32) for _ in range(2)]
    sga = AF1.get([512], F32)
    sgb = AF1.get([512], F32)
    t1 = AF1.get([512], F32)
    t2 = AF1.get([512], F32)
    f1_end = AF1.cur
    WO = AF1.get([8, 1024], BF16)
    gfr = AF1.get([1024], F32)
    wo_end = AF1.cur
    wg_v = CONV["wg"].rearrange("(c p) n -> p c n", p=128)
    for blk in range(2):
        P.dma("sync", WG[:, :, blk * 512:(blk + 1) * 512], wg_v[:, :, blk * 512:(blk + 1) * 512], ["wg_bf"], [f"WGa{blk}"])
        P.dma("sync", WG[:, :, 1024 + blk * 512:1024 + (blk + 1) * 512], wg_v[:, :, 1024 + blk * 512:1024 + (blk + 1) * 512], ["wg_bf"], [f"WGb{blk}"])
    P.dma("sync", WBF[:, :, :], CONV["wbf"].rearrange("(c p) n -> p c n", p=128), ["wbf_bf"], ["WBF"])
    P.dma("sync", WBN[:, :, :], CONV["wbn"].rearrange("(c p) n -> p c n", p=128), ["wbn_bf"], ["WBN"])
    P.dma("sync", WO[:, :, :], CONV["wo"].rearrange("(c p) n -> p c n", p=128), ["wo_bf"], ["WO"])
    P.dma("sync", gfr, D["gfin"].partition_broadcast(128)[:, 0, :], [], ["gfr"])
    def nt_f1(stl_, tts=(0, 1, 2, 3)):
        for tt in tts:
            j = stl_ * 4 + tt
            b = j % 2
            norm_transpose(xo_v[j], xt1[b], sq1, ss1[b], rs1[b], xn1[b], hT4[stl_ % 2][:, :, tt * 128:(tt + 1) * 128], gat, f"F{b}", 7, [],
                           f"hF{stl_ % 2}")

    nt_f1(0)
    for stl in range(4):
        hT = hT4[stl % 2]
        hkey = f"hF{stl % 2}"
        tsl = slice(stl * 512, (stl + 1) * 512)
        for c in range(8):
            for d in range(8):
                P.mm(bankv(0)[:, 0:512], WG[:, d, c * 128:(c + 1) * 128], hT[:, d, :], d == 0, d == 7, [f"WGa{c // 4}", hkey], ["B0"])
            for d in range(8):
                P.mm(bankv(1)[:, 0:512], WG[:, d, 1024 + c * 128:1024 + (c + 1) * 128], hT[:, d, :], d == 0, d == 7, [f"WGb{c // 4}", hkey], ["B1"])
            for d in range(4):
                P.mm(bankv(2)[:, 0:512], WBF[:, d, c * 128:(c + 1) * 128], OTF[:, d, tsl], d == 0, d == 3, ["WBF", "OTF"], ["B2"])
            for d in range(4):
                P.mm(bankv(3)[:, 0:512], WBN[:, d, c * 128:(c + 1) * 128], OTN[:, d, tsl], d == 0, d == 3, ["WBN", "OTN"], ["B3"])
            if stl + 1 < 4 and c < 4:
                nt_f1(stl + 1, (c,))
            P.op("scalar", I.activation(sga, bankv(0)[:, 0:512], AF.Sigmoid), ["B0"], ["sga"])
            P.op("scalar", I.activation(sgb, bankv(1)[:, 0:512], AF.Sigmoid), ["B1"], ["sgb"])
            P.op("vector", I.tensor_tensor(t1, bankv(2)[:, 0:512], sga, ALU.mult), ["B2", "sga"], ["t1"])
            P.op("vector", I.tensor_tensor(t2, bankv(3)[:, 0:512], sgb, ALU.mult), ["B3", "sgb"], ["t2"])
            P.op("gpsimd", I.tensor_tensor(MT[:, c, tsl], t1, t2, ALU.add), ["t1", "t2"], [("MT", stl)])

    dump("MT", MT, [128, 8, 2048], [("MT", i) for i in range(4)], BF16)
    phase_end("F1")
    P.barrier()
    AF2a = Alloc(misc_end, f1_end)
    AF2b = Alloc(wo_end, ebw_lo)
    ATt = AF2a.get([32, 512], BF16)
    x1s = AF2a.get([4, 1024], F32)
    FF1 = [AF2a.get([8, 512], BF16) for _ in range(3)]
    h2T = AF2a.get([8, 512], BF16)
    FF2 = [AF2b.get([4, 1024], BF16) for _ in range(3)]
    xt2 = [AF2b.get([1024], F32) for _ in range(2)]
    xn2 = [AF2b.get([1024], BF16) for _ in range(2)]
    ss2 = [AF2b.get([1], F32) for _ in range(2)]
    rs2 = [AF2b.get([1], F32) for _ in range(2)]
    sqr = [AF2b.get([512], F32) for _ in range(2)]
    oto = [AF2b.get([1024], F32) for _ in range(2)]
    wf1_v = CONV["wf1"].rearrange("(c p) n -> p c n", p=128)
    wf2_v = CONV["wf2"].rearrange("(c p) n -> p c n", p=128)
    ff1_ctr = [0]
    ff2_ctr = [0]
    for stl in range(4):
        tsl = slice(stl * 512, (stl + 1) * 512)
        for tt in range(4):
            j = stl * 4 + tt
            b = j % 2
            t0 = stl * 512 + tt * 128
            if not (stl > 0 and tt < 2):
                P.dma("sync", xt2[b], xo_v[j], [], [f"xt2{b}"])
            for half in range(2):
                bk = tt * 2 + half
                for d in range(8):
                    P.mm(bankv(bk)[:, 0:512], MT[:, d, t0:t0 + 128], WO[:, d, half * 512:(half + 1) * 512], d == 0, d == 7,
                         [("MT", stl), "WO"], [f"B{bk}"])
                P.op("vector", I.tensor_tensor(
                    x1s[:, tt, half * 512:(half + 1) * 512], bankv(bk)[:, 0:512], xt2[b][:, half * 512:(half + 1) * 512], ALU.add),
                    [f"B{bk}", f"xt2{b}"], [("x1", tt)])
        for tt in range(4):
            b = tt % 2
            P.op("scalar", I.activation(xn2[b], x1s[:, tt, :], AF.Square, accum_out=ss2[b]), [("x1", tt)], [f"xn2{b}", f"ss2{b}"])
            P.op("scalar", I.activation(rs2[b], ss2[b], AF.Ln, bias=EPS_AP, scale=1.0 / 1024), [f"ss2{b}", "epsap"], [f"rs2{b}"])
            P.op("scalar", I.activation(rs2[b], rs2[b], AF.Exp, scale=-0.5), [f"rs2{b}"], [f"rs2{b}"])
            P.op("gpsimd", I.tensor_scalar(xn2[b], x1s[:, tt, :], rs2[b], 1.0, ALU.mult, ALU.mult), [("x1", tt), f"rs2{b}"], [f"xn2{b}"])
            tb = tt % 2
            pb = bankv(tb, BF16)
            for d in range(8):
                P.tr(pb[:, d * 128:(d + 1) * 128], xn2[b][:, d * 128:(d + 1) * 128], identb, [f"xn2{b}", "identb"], [f"B{tb}"])
            P.op("vector", I.tensor_tensor(h2T[:, :, tt * 128:(tt + 1) * 128], pb.rearrange("p (d t) -> p d t", d=8),
                                           gml.rearrange("p (d o) -> p d o", o=1).broadcast_to([128, 8, 128]), ALU.mult),
                 [f"B{tb}", "gml"], ["h2T"])
        for fc in range(8):
            wb = ff1_ctr[0] % 3
            ff1_ctr[0] += 1
            for d in range(8):
                P.dma("sync", FF1[wb][:, d, :], wf1_v[:, d, fc * 512:(fc + 1) * 512], ["wf1_bf"], [f"FF1{wb}"])
            for q4 in range(4):
                ffc = fc * 4 + q4
                bk = 2 + (ffc % 4)
                for d in range(8):
                    P.mm(bankv(bk)[:, 0:512], FF1[wb][:, d, q4 * 128:(q4 + 1) * 128], h2T[:, d, :], d == 0, d == 7,
                         [f"FF1{wb}", "h2T"], [f"B{bk}"])
                sq_ = sqr[ffc % 2]
                P.op("scalar", I.activation(sq_, bankv(bk)[:, 0:512], AF.Square), [f"B{bk}"], [f"sqr{ffc % 2}"])
                P.op("vector", I.scalar_tensor_tensor(ATt[:, ffc, :], bankv(bk)[:, 0:512], 0.0, sq_, ALU.is_gt, ALU.mult),
                     [f"B{bk}", f"sqr{ffc % 2}"], ["ATt"])
        for fc in range(8):
            wb = ff2_ctr[0] % 3
            ff2_ctr[0] += 1
            for q4 in range(4):
                P.dma("sync", FF2[wb][:, q4, :], wf2_v[:, fc * 4 + q4, :], ["wf2_bf"], [f"FF2{wb}"])
            for tt in range(4):
                for q4 in range(4):
                    ffc = fc * 4 + q4
                    for half in range(2):
                        P.mm(bankv(tt * 2 + half)[:, 0:512], ATt[:, ffc, tt * 128:(tt + 1) * 128], FF2[wb][:, q4, half * 512:(half + 1) * 512],
                             ffc == 0, ffc == 31, ["ATt", f"FF2{wb}"], [f"B{tt * 2 + half}"])
        if stl + 1 < 4:
            for tt_ in range(2):
                P.dma("sync", xt2[tt_], xo_v[(stl + 1) * 4 + tt_], [], [f"xt2{tt_}"])
        for tt in range(4):
            j = stl * 4 + tt
            b = j % 2
            for half in range(2):
                P.op("vector", I.tensor_tensor(
                    x1s[:, tt, half * 512:(half + 1) * 512], bankv(tt * 2 + half)[:, 0:512], x1s[:, tt, half * 512:(half + 1) * 512], ALU.add),
                    [f"B{tt * 2 + half}", ("x1", tt)], [("x1", tt)])
            P.op("scalar", I.activation(oto[b], x1s[:, tt, :], AF.Square, accum_out=ss2[b]), [("x1", tt)], [f"oto{b}", f"ss2{b}"])
            P.op("scalar", I.activation(rs2[b], ss2[b], AF.Ln, bias=EPS_AP, scale=1.0 / 1024), [f"ss2{b}", "epsap"], [f"rs2{b}"])
            P.op("scalar", I.activation(rs2[b], rs2[b], AF.Exp, scale=-0.5), [f"rs2{b}"], [f"rs2{b}"])
            P.op("vector", I.scalar_tensor_tensor(oto[b], x1s[:, tt, :], rs2[b], gfr, ALU.mult, ALU.mult),
                 [("x1", tt), f"rs2{b}", "gfr"], [f"oto{b}"])
            P.dma("sync", out[j * 128:(j + 1) * 128, :], oto[b], [f"oto{b}"], [])

    return


_CACHE = {}


def _prep_inputs(inp, core):
    b, par = core // 2, core % 2
    f = lambda a: np.ascontiguousarray(np.asarray(a, dtype=np.float32))
    x = f(inp["x"])
    w_in = f(inp["w_in"])[0]
    o = np.cumsum([0, 512, 512, 512, 8, 512, 128, 128, 128, 128, 128, 128, 24, 1024, 1024])
    fq, fk, fv, flog, nq, kc, vc, ksl, vsl, kwn, vwn, ngate, ga, gb = [w_in[:, o[i]:o[i + 1]] for i in range(14)]
    nq_pairs = np.concatenate([np.concatenate([nq[:, h * 64:(h + 1) * 64], nq[:, (h + 4) * 64:(h + 5) * 64]], axis=1) for h in range(4)], axis=1)
    own = np.concatenate([np.arange((2 * g + par) * 128, (2 * g + par + 1) * 128) for g in range(16)])
    d = dict(
        xf=x[b], xo=x[b][own],
        wkv=np.concatenate([fk, kc, vc, ksl, kwn, fv, vsl, vwn, flog], axis=1),
        wq=np.concatenate([fq, nq_pairs, ngate], axis=1),
        wg=np.concatenate([ga, gb], axis=1),
        gat=f(inp["g_attn"])[0].reshape(8, 128).T, gml=f(inp["g_mlp"])[0].reshape(8, 128).T,
        gfin=f(inp["g_final"]).reshape(1, 1024), bfg=f(inp["b_forget"]).reshape(1, 8), tbl=f(inp["rel_bias_table"]),
        pek=f(inp["cmp_pe_k"])[0], pev=f(inp["cmp_pe_v"])[0], w1k=f(inp["cmp_w1_k"])[0], w1v=f(inp["cmp_w1_v"])[0],
        w2k=f(inp["cmp_w2_k"])[0], w2v=f(inp["cmp_w2_v"])[0],
        wbf=f(inp["w_br_fox"])[0], wbn=f(inp["w_br_nsa"])[0], wo=f(inp["w_out"])[0],
        wf1=f(inp["w_ff1"])[0], wf2=f(inp["w_ff2"])[0],
    )
    d.update(make_consts(par))
    return {k: np.ascontiguousarray(v, dtype=np.float32) for k, v in d.items()}, own


def kernel(**inputs):
    nc = bass.Bass("TRN2", target_bir_lowering=False)
    build(nc)
    in_maps, owns = [], []
    for core in range(8):
        m, own = _prep_inputs(inputs, core)
        in_maps.append(m)
        owns.append(own)
    res = run_bass_kernel_spmd(nc, in_maps, core_ids=list(range(8)))
    outp = np.zeros((4, 4096, 1024), np.float32)
    for core in range(8):
        outp[core // 2, owns[core]] = res.results[core]["out"]
    return outp
```
